# Optimizing a Trainium2 kernel written in Bass

```python
import jax, jax.numpy as jnp
from jax import lax
import numpy as np

D_MODEL = 1024
BATCH = 16
SEQ = 4096
DEPTH = 4

N_HEADS = 16
HEAD_DIM = D_MODEL // N_HEADS
D_FF = -(-8 * D_MODEL // (3 * 256)) * 256
N_A_LAYERS = DEPTH // 2
N_B_LAYERS = DEPTH - N_A_LAYERS
BLOCK = 128
ROPE_THETA = 10000.0
NORM_EPS = 1e-6
DILATED_BRANCHES = ((128, 1), (512, 4), (2048, 16))

kernel_name = "yoco_stickbreak_dilated_hybrid"


def _rmsnorm(h, g):
    hf = h.astype(jnp.float32)
    y = hf * lax.rsqrt(jnp.mean(hf * hf, axis=-1, keepdims=True) + NORM_EPS)
    return (y * g.astype(jnp.float32)).astype(h.dtype)


def _rope(a):
    s, d = a.shape[1], a.shape[3]
    inv_freq = ROPE_THETA ** (-jnp.arange(0, d, 2, dtype=jnp.float32) / d)
    ang = jnp.arange(s, dtype=jnp.float32)[:, None] * inv_freq[None, :]
    cos = jnp.cos(ang)[:, None, :]
    sin = jnp.sin(ang)[:, None, :]
    af = a.astype(jnp.float32)
    a1, a2 = af[..., : d // 2], af[..., d // 2:]
    return jnp.concatenate([a1 * cos - a2 * sin, a2 * cos + a1 * sin], axis=-1).astype(a.dtype)


def _swiglu(h, w_gate, w_up, w_down):
    return (jax.nn.silu(h @ w_gate) * (h @ w_up)) @ w_down


def _stick_breaking(q, k, v):
    b, s, h, d = q.shape
    nq = s // BLOCK
    scale = d ** -0.5
    kh = k.transpose(0, 2, 1, 3)
    vh = v.transpose(0, 2, 1, 3)
    qblk = q.reshape(b, nq, BLOCK, h, d).transpose(1, 0, 3, 2, 4)
    key_pos = jnp.arange(s)

    def one_block(args):
        qb, bi = args
        qpos = bi * BLOCK + jnp.arange(BLOCK)
        strict = key_pos[None, :] < qpos[:, None]
        z = jnp.einsum('bhqd,bhkd->bhqk', qb, kh).astype(jnp.float32) * scale
        log_stay = jnp.where(strict, jax.nn.log_sigmoid(-z), 0.0)
        log_after = lax.cumsum(log_stay, axis=3, reverse=True) - log_stay
        w = jnp.where(strict, jnp.exp(jax.nn.log_sigmoid(z) + log_after), 0.0)
        return jnp.einsum('bhqk,bhkd->bhqd', w, vh.astype(jnp.float32)).astype(q.dtype)

    out = lax.map(one_block, (qblk, jnp.arange(nq)))
    return out.transpose(1, 0, 3, 2, 4).reshape(b, s, h, d)


def _to_blocks(a, r):
    b, s, h, d = a.shape
    n = s // r
    nb = -(-n // BLOCK)
    a = a.reshape(b, n, r, h, d).transpose(0, 2, 3, 1, 4)
    a = jnp.pad(a, ((0, 0), (0, 0), (0, 0), (0, nb * BLOCK - n), (0, 0)))
    return a.reshape(b, r, h, nb, BLOCK, d)


def _from_blocks(a, s):
    b, r, h, nb, blk, e = a.shape
    n = s // r
    a = a.reshape(b, r, h, nb * blk, e)[:, :, :, :n]
    return a.transpose(0, 3, 1, 2, 4).reshape(b, s, h, e)


def _dilated_branch(qb, kb, vb, steps):
    nb = qb.shape[3]
    scale = qb.shape[-1] ** -0.5
    k_prev = jnp.concatenate([jnp.zeros_like(kb[:, :, :, :1]), kb[:, :, :, :-1]], axis=3)
    v_prev = jnp.concatenate([jnp.zeros_like(vb[:, :, :, :1]), vb[:, :, :, :-1]], axis=3)
    k_band = jnp.concatenate([k_prev, kb], axis=4)
    v_band = jnp.concatenate([v_prev, vb], axis=4)
    z = jnp.einsum('brhnqd,brhnkd->brhnqk', qb, k_band).astype(jnp.float32) * scale
    qi = jnp.arange(BLOCK)[:, None]
    kj = jnp.arange(2 * BLOCK)[None, :]
    dist = BLOCK + qi - kj
    key_idx = (jnp.arange(nb)[:, None, None] - 1) * BLOCK + kj[None]
    valid = (dist >= 0)[None] & (dist <= steps)[None] & (key_idx >= 0)
    z = jnp.where(valid, z, -jnp.inf)
    m = jnp.max(z, axis=-1, keepdims=True)
    p = jnp.exp(z - m)
    den = jnp.sum(p, axis=-1, keepdims=True)
    o = jnp.einsum('brhnqk,brhnkd->brhnqd', p, v_band.astype(jnp.float32)) / den
    lse = m + jnp.log(den)
    return o, lse


def _dilated_attention(q, kv_blocks):
    s = q.shape[1]
    outs, lses = [], []
    for (window, r), (kb, vb) in zip(DILATED_BRANCHES, kv_blocks):
        o, lse = _dilated_branch(_to_blocks(q, r), kb, vb, window // r)
        outs.append(_from_blocks(o, s))
        lses.append(_from_blocks(lse, s))
    wts = jax.nn.softmax(jnp.stack(lses, axis=0), axis=0)
    return jnp.sum(wts * jnp.stack(outs, axis=0), axis=0).astype(q.dtype)


def setup_inputs(seed: int = 0) -> dict:
    key = jax.random.key(seed)
    ks = jax.random.split(key, 14)
    d, f = D_MODEL, D_FF
    nrm = lambda k, shape, fan: jax.random.normal(k, shape, jnp.float32) * fan ** -0.5
    gain = lambda k, shape: 1.0 + 0.02 * jax.random.normal(k, shape, jnp.float32)
    return {
        "x": jax.random.normal(ks[0], (BATCH, SEQ, d), jnp.float32),
        "norm_mix": gain(ks[1], (DEPTH, d)),
        "w_qkv_a": nrm(ks[2], (N_A_LAYERS, d, 3 * d), d),
        "w_o_a": nrm(ks[3], (N_A_LAYERS, d, d), d),
        "norm_kv": gain(ks[4], (d,)),
        "w_kv": nrm(ks[5], (d, 2 * d), d),
        "w_q_b": nrm(ks[6], (N_B_LAYERS, d, d), d),
        "w_o_b": nrm(ks[7], (N_B_LAYERS, d, d), d),
        "norm_ffn": gain(ks[8], (DEPTH, d)),
        "w_gate": nrm(ks[9], (DEPTH, d, f), d),
        "w_up": nrm(ks[10], (DEPTH, d, f), d),
        "w_down": nrm(ks[11], (DEPTH, f, d), f),
        "norm_final": gain(ks[12], (d,)),
    }


def reference(x, norm_mix, w_qkv_a, w_o_a, norm_kv, w_kv, w_q_b, w_o_b,
              norm_ffn, w_gate, w_up, w_down, norm_final):
    b, s, d = x.shape
    h = x
    kv_blocks = None
    for layer in range(DEPTH):
        hn = _rmsnorm(h, norm_mix[layer])
        if layer < N_A_LAYERS:
            q, k, v = jnp.split(hn @ w_qkv_a[layer], 3, axis=-1)
            q = q.reshape(b, s, N_HEADS, HEAD_DIM)
            k = k.reshape(b, s, N_HEADS, HEAD_DIM)
            v = v.reshape(b, s, N_HEADS, HEAD_DIM)
            o = _stick_breaking(q, k, v).reshape(b, s, d)
            h = h + o @ w_o_a[layer]
        else:
            j = layer - N_A_LAYERS
            if layer == N_A_LAYERS:
                kv = _rmsnorm(h, norm_kv) @ w_kv
                ks_, vs_ = jnp.split(kv, 2, axis=-1)
                ks_ = _rope(ks_.reshape(b, s, N_HEADS, HEAD_DIM))
                vs_ = vs_.reshape(b, s, N_HEADS, HEAD_DIM)
                kv_blocks = tuple((_to_blocks(ks_, r), _to_blocks(vs_, r))
                                  for (_, r) in DILATED_BRANCHES)
            q = _rope((hn @ w_q_b[j]).reshape(b, s, N_HEADS, HEAD_DIM))
            o = _dilated_attention(q, kv_blocks).reshape(b, s, d)
            h = h + o @ w_o_b[j]
        h = h + _swiglu(_rmsnorm(h, norm_ffn[layer]), w_gate[layer], w_up[layer], w_down[layer])
    return _rmsnorm(h, norm_final)
```

```python
import numpy as np
import ml_dtypes
from contextlib import ExitStack
import concourse.bass as bass
import concourse.mybir as mybir
from concourse.bass_utils import run_bass_kernel_spmd

F32 = mybir.dt.float32
BF16 = mybir.dt.bfloat16
AF = mybir.ActivationFunctionType
ALU = mybir.AluOpType

D = 1024
S = 4096
H = 16
HD = 64
FF = 2816
NFC = 8
NJ = 22
DEPTH = 4
NCORES = 8
EPS = 1e-6
NEG = -30000.0
BRANCHES = (1, 4, 16)


class Prog:
    CE = ("pe", "act", "dve", "pool")

    def __init__(self, nc, st):
        self.nc = nc
        self.sem = {e: st.enter_context(nc.semaphore("s_" + e)) for e in self.CE}
        self.cnt = {e: 0 for e in self.CE}
        self.dq = {
            "sp": [st.enter_context(nc.semaphore(f"d_sp{i}")) for i in range(8)],
            "pool": [st.enter_context(nc.semaphore(f"d_pl{i}")) for i in range(4)],
        }
        self.dcnt = {"sp": [0] * 8, "pool": [0] * 4}
        self.dk = {"sp": 0, "pool": 0}
        self.reset()

    def reset(self):
        self.ops = []
        self.last_w = {}
        self.readers = {}

    def op(self, eng, fn, reads=(), writes=(), dma=False):
        i = len(self.ops)
        deps = set()
        for r in reads:
            if r in self.last_w:
                deps.add(self.last_w[r])
        for w in writes:
            if w in self.last_w:
                deps.add(self.last_w[w])
            deps.update(self.readers.get(w, ()))
        for r in reads:
            self.readers.setdefault(r, []).append(i)
        for w in writes:
            self.last_w[w] = i
            self.readers[w] = []
        deps.discard(i)
        self.ops.append(dict(eng=eng, fn=fn, deps=deps, dma=dma))
        return i

    def dma(self, q, out, in_, reads=(), writes=()):
        return self.op(q, lambda e: e.dma_start(out=out, in_=in_), reads, writes, dma=True)

    def flush(self, final=False):
        ops = self.ops
        n = len(ops)
        has_dep = [False] * n
        for o in ops:
            for d in o["deps"]:
                has_dep[d] = True
        last = {}
        for i, o in enumerate(ops):
            if not o["dma"]:
                last[o["eng"]] = i
        for i in last.values():
            has_dep[i] = True
        tok = [None] * n
        pre_wait = [None] * n
        cnt = dict(self.cnt)
        dk = dict(self.dk)
        dcnt = {k: list(v) for k, v in self.dcnt.items()}
        for i, o in enumerate(ops):
            if o["dma"]:
                q = o["eng"]
                R = len(self.dq[q])
                si = dk[q] % R
                dk[q] += 1
                prev = dcnt[q][si]
                dcnt[q][si] += 1
                tok[i] = (self.dq[q][si], 16 * dcnt[q][si])
                if prev > 0:
                    pre_wait[i] = (self.dq[q][si], 16 * prev)
            elif has_dep[i]:
                e = o["eng"]
                cnt[e] += 1
                tok[i] = (self.sem[e], cnt[e])
        start_tokens = [(self.sem[e], self.cnt[e]) for e in self.CE if self.cnt[e] > 0]
        for q in self.dq:
            for s_, c_ in zip(self.dq[q], self.dcnt[q]):
                if c_ > 0:
                    start_tokens.append((s_, 16 * c_))
        end_tokens = []
        for q in self.dq:
            for s_, c_ in zip(self.dq[q], dcnt[q]):
                if c_ > 0:
                    end_tokens.append((s_, 16 * c_))

        with self.nc.Block() as blk:
            for qname, deco in (("pe", blk.tensor), ("act", blk.scalar), ("dve", blk.vector),
                                ("pool", blk.gpsimd), ("sp", blk.sync)):
                def body(e, qname=qname):
                    waited = {}

                    def wait(t):
                        s_, v = t
                        k = id(s_)
                        if waited.get(k, 0) >= v:
                            return
                        waited[k] = v
                        e.wait_ge(s_, v)

                    for t in start_tokens:
                        wait(t)
                    for i, o in enumerate(ops):
                        if o["eng"] != qname:
                            continue
                        for d in sorted(o["deps"]):
                            od = ops[d]
                            if qname == "pe" and od["eng"] == "pe" and not od["dma"]:
                                continue
                            wait(tok[d])
                        if pre_wait[i] is not None:
                            wait(pre_wait[i])
                        ins = o["fn"](e)
                        if tok[i] is not None:
                            ins.then_inc(tok[i][0], 16 if o["dma"] else 1)
                    if final and qname == "sp":
                        for t in end_tokens:
                            wait(t)
                        for ce in self.CE:
                            if cnt[ce] > 0:
                                wait((self.sem[ce], cnt[ce]))
                deco(body)
        self.cnt = cnt
        self.dk = dk
        self.dcnt = dcnt
        self.reset()


def _consts():
    bf = ml_dtypes.bfloat16
    j = np.arange(128)[:, None]
    t = np.arange(128)[None, :]
    ident = np.eye(128, dtype=np.float32).astype(bf)
    triu = (j >= t).astype(np.float32).astype(bf)
    comp = (j < t).astype(np.float32).astype(bf)
    maskneg = np.where(j < t, 0.0, NEG).astype(np.float32).astype(bf)
    prev = (j >= t).astype(np.float32)
    cur = (j <= t).astype(np.float32)
    m01 = np.concatenate([cur, prev, cur, prev], axis=1).astype(bf)
    m01f = np.concatenate([np.zeros_like(prev), cur, prev, cur], axis=1).astype(bf)
    inv_freq = (10000.0 ** (-np.arange(0, HD, 2, dtype=np.float32) / HD)).astype(np.float32)
    ang = (np.arange(S, dtype=np.float32)[:, None] * inv_freq[None, :]).astype(np.float32)
    cos = np.cos(ang).astype(np.float32).T
    sin = np.sin(ang).astype(np.float32).T
    cos64 = np.concatenate([cos, cos], axis=0)
    sin64 = np.concatenate([-sin, sin], axis=0)
    cos2 = np.ascontiguousarray(np.concatenate([cos64, cos64], axis=0))
    sin2 = np.ascontiguousarray(np.concatenate([sin64, sin64], axis=0))
    shift = np.zeros((128, 64), np.float32)
    shift[np.arange(64) + 64, np.arange(64)] = 1.0
    return dict(c_ident=ident, c_triu=triu, c_comp=comp, c_maskneg=maskneg, c_m01=m01, c_m01f=m01f,
                c_cos=cos2, c_sin=sin2, c_shift=shift)


_CONST_SPECS = dict(c_ident=([128, 128], BF16), c_triu=([128, 128], BF16), c_comp=([128, 128], BF16),
                    c_maskneg=([128, 128], BF16), c_m01=([128, 512], BF16), c_m01f=([128, 512], BF16),
                    c_cos=([128, S], F32), c_sin=([128, S], F32), c_shift=([128, 64], F32))


def build_program(nseq=2, debug=False, stop_after=None):
    T = nseq * S
    nc = bass.Bass("TRN2", target_bir_lowering=False)
    I = {}

    def din(name, shape, dt=F32):
        I[name] = nc.dram_tensor(name, list(shape), dt, kind="ExternalInput").ap()

    din("x", [T, D])
    din("norm_mix", [DEPTH, D])
    din("w_qkv_a", [2, D, 3 * D])
    din("w_o_a", [2, D, D])
    din("norm_kv", [D])
    din("w_kv", [D, 2 * D])
    din("w_q_b", [2, D, D])
    din("w_o_b", [2, D, D])
    din("norm_ffn", [DEPTH, D])
    din("w_gate", [DEPTH, D, FF])
    din("w_up", [DEPTH, D, FF])
    din("w_down", [DEPTH, FF, D])
    din("norm_final", [D])
    for k, (shp, dt) in _CONST_SPECS.items():
        din(k, shp, dt)
    y = nc.dram_tensor("y", [T, D], F32, kind="ExternalOutput").ap()
    skind = "ExternalOutput" if debug else "Internal"

    def scratch(name, shape, dt):
        return nc.dram_tensor(name, list(shape), dt, kind=skind).ap()

    hbuf = scratch("s_h", [T, D], F32)
    hnT = scratch("s_hnT", [D, T], BF16)
    hkvT = scratch("s_hkvT", [D, T], BF16)
    qT = scratch("s_qT", [D, T], BF16)
    kT = scratch("s_kT", [D, T], BF16)
    vv = scratch("s_v", [T, D], BF16)
    oT = scratch("s_oT", [D, T], BF16)

    gst = ExitStack()
    with gst, nc.allow_low_precision("bf16 matmul operands, fp32 accumulate"):
        P = Prog(nc, gst)
        ps = [gst.enter_context(nc.psum_tensor(f"ps{i}", [128, 512], F32)) for i in range(8)]
        psb = [p.bitcast(BF16) for p in ps]
        ident = gst.enter_context(nc.sbuf_tensor("ident", [128, 128], BF16))
        rstd_eps = 1024.0 * EPS

        uid = [0]

        def T_(st, name, shape, dt):
            uid[0] += 1
            return st.enter_context(nc.sbuf_tensor(f"{name}_u{uid[0]}", list(shape), dt))

        first = [True]

        def load_ident():
            if first[0]:
                P.dma("sp", ident[:], I["c_ident"], writes=["ident"])
                first[0] = False

        def bcast_row(ap1d):
            return ap1d.partition_broadcast(128)

        class NormT:
            def __init__(self, st, tag, nb, ngain):
                self.tag = tag
                self.nb = nb
                self.hn = [T_(st, f"{tag}_hn{g}", [128, nb, D], BF16) for g in range(ngain)]
                self.junk = T_(st, f"{tag}_junk", [128, D], BF16) if ngain == 0 else None
                self.ss = T_(st, f"{tag}_ss", [128, 8], F32)
                self.rstd = T_(st, f"{tag}_rstd", [128, 8], F32)
                self.k = 0

            def stats(self, b, src, src_res):
                tag = self.tag
                ss, rstd = self.ss, self.rstd
                if self.junk is not None:
                    junk, jres = self.junk[:], f"{tag}_junk"
                else:
                    junk, jres = self.hn[0][:, b, :], f"{tag}_hn0_{b}"
                P.op("dve", lambda e: e.memset(ss[:, b:b + 1], 0.0), writes=[f"{tag}_ss{b}"])
                P.op("act", lambda e: e.activation(out=junk, in_=src, func=AF.Square, accum_out=ss[:, b:b + 1]),
                     reads=[src_res], writes=[jres, f"{tag}_ss{b}"])

            def finish(self):
                tag, nb = self.tag, self.nb
                ss, rstd = self.ss, self.rstd
                P.op("act", lambda e: e.activation(out=rstd[:, 0:nb], in_=ss[:, 0:nb], func=AF.Ln, bias=rstd_eps),
                     reads=[f"{tag}_ss{b}" for b in range(nb)], writes=[f"{tag}_rstd{b}" for b in range(nb)])
                P.op("act", lambda e: e.activation(out=rstd[:, 0:nb], in_=rstd[:, 0:nb], func=AF.Exp, scale=-0.5),
                     reads=[f"{tag}_rstd{b}" for b in range(nb)], writes=[f"{tag}_rstd{b}" for b in range(nb)])

            def scale(self, b, src, src_res, gi, gain, gain_res):
                tag = self.tag
                hn, rstd = self.hn[gi], self.rstd
                P.op("dve", lambda e: e.scalar_tensor_tensor(out=hn[:, b, :], in0=src, scalar=rstd[:, b:b + 1],
                                                              in1=gain[:], op0=ALU.mult, op1=ALU.mult),
                     reads=[src_res, f"{tag}_rstd{b}", gain_res], writes=[f"{tag}_hn{gi}_{b}"])

            def transpose_group(self, gi, fc, dst, dst_res, banks=(2, 3)):
                tag, nb = self.tag, self.nb
                hn = self.hn[gi]
                bank = banks[self.k % len(banks)]
                self.k += 1
                sres = f"ps{bank}"
                for b in range(nb):
                    P.op("pe", lambda e, b=b: e.transpose(
                        psb[bank][:, b * 128:(b + 1) * 128], hn[:, b, fc * 128:(fc + 1) * 128], ident[:]),
                        reads=[f"{tag}_hn{gi}_{b}", "ident"], writes=[sres])
                if fc % 2 == 0:
                    P.op("act", lambda e: e.copy(dst[:, fc, :], psb[bank][:, 0:nb * 128]),
                         reads=[sres], writes=[dst_res])
                else:
                    P.op("dve", lambda e: e.tensor_copy(dst[:, fc, :], psb[bank][:, 0:nb * 128]),
                         reads=[sres], writes=[dst_res])

            def transpose(self, gi, dst, dst_res, banks):
                tag, nb = self.tag, self.nb
                hn = self.hn[gi]
                for fc in range(NFC):
                    bank = banks[self.k % len(banks)]
                    self.k += 1
                    for b in range(nb):
                        P.op("pe", lambda e, b=b, fc=fc, bank=bank: e.transpose(
                            psb[bank][:, b * 128:(b + 1) * 128], hn[:, b, fc * 128:(fc + 1) * 128], ident[:]),
                            reads=[f"{tag}_hn{gi}_{b}", "ident"], writes=[f"ps{bank}"])
                    eng = "act" if fc % 2 == 0 else "dve"
                    if eng == "act":
                        P.op("act", lambda e, fc=fc, bank=bank: e.copy(dst[:, fc, :], psb[bank][:, 0:nb * 128]),
                             reads=[f"ps{bank}"], writes=[dst_res])
                    else:
                        P.op("dve", lambda e, fc=fc, bank=bank: e.tensor_copy(dst[:, fc, :], psb[bank][:, 0:nb * 128]),
                             reads=[f"ps{bank}"], writes=[dst_res])

        def fcview(ap2d):
            return ap2d.rearrange("(c p) n -> p c n", p=128)

        def load_w(st, name, src, nchunk, ncol, c0=0):
            w = T_(st, name, [128, nchunk, ncol], BF16)
            for c in range(nchunk):
                P.dma("pool", w[:, c, :], src[c * 128:(c + 1) * 128, c0:c0 + ncol], writes=[name])
            return w

        def load_w_swapped(st, name, src, c0):
            w = T_(st, name, [128, NFC, D], BF16)
            for c in range(NFC):
                sv = src[c * 128:(c + 1) * 128, c0:c0 + D].rearrange("k (h two d) -> k h two d", two=2, d=32)
                dv = w[:, c, :].rearrange("p (h two d) -> p h two d", two=2, d=32)
                P.dma("pool", dv[:, :, 0, :], sv[:, :, 1, :], writes=[name])
                P.dma("pool", dv[:, :, 1, :], sv[:, :, 0, :], writes=[name])
            return w

        def phase_pre():
            with ExitStack() as st:
                load_ident()
                g = T_(st, "pre_g", [128, D], F32)
                P.dma("sp", g[:], bcast_row(I["norm_mix"][0]), writes=["pre_g"])
                P.op("dve", lambda e: e.tensor_scalar(g[:], g[:], 32.0, None, ALU.mult), reads=["pre_g"], writes=["pre_g"])
                NB = 4
                nt = NormT(st, "pre", NB, 1)
                xb = [T_(st, f"pre_x{i}", [128, NB, D], F32) for i in range(2)]
                ob = [T_(st, f"pre_o{i}", [128, NFC, NB * 128], BF16) for i in range(2)]
                nchp = T // (NB * 128)

                def pre_load(c):
                    t0 = c * NB * 128
                    P.dma("sp", xb[c % 2][:], I["x"][t0:t0 + NB * 128, :].rearrange("(b p) d -> p b d", p=128),
                          writes=[f"pre_x{c % 2}"])

                pre_load(0)
                for c in range(nchp):
                    t0 = c * NB * 128
                    x_, o_ = xb[c % 2], ob[c % 2]
                    xr, orr = f"pre_x{c % 2}", f"pre_o{c % 2}"
                    if c + 1 < nchp:
                        pre_load(c + 1)
                    for b in range(NB):
                        nt.stats(b, x_[:, b, :], xr)
                    nt.finish()
                    for b in range(NB):
                        nt.scale(b, x_[:, b, :], xr, 0, g, "pre_g")
                    nt.transpose(0, o_, orr, [0, 1])
                    P.dma("sp", fcview(hnT)[:, :, t0:t0 + NB * 128], o_[:], reads=[orr])
                P.flush()

        def phase_qkv_a(L):
            with ExitStack() as st:
                W = load_w(st, "qa_w", I["w_qkv_a"][L], NFC, 3 * D)
                NB = 4
                TB = NB * 128
                hb = [T_(st, f"qa_h{i}", [128, NFC, TB], BF16) for i in range(2)]
                qo = [T_(st, f"qa_q{i}", [128, 8, TB], BF16) for i in range(2)]
                ko = [T_(st, f"qa_k{i}", [128, 8, TB], BF16) for i in range(2)]
                vo = [T_(st, f"qa_v{i}", [128, NB, D], BF16) for i in range(2)]
                k = 0

                def qa_load(c):
                    P.dma("sp", hb[c % 2][:], fcview(hnT)[:, :, c * TB:(c + 1) * TB], writes=[f"qa_h{c % 2}"])

                qa_load(0)
                for c in range(T // TB):
                    t0 = c * TB
                    i2 = c % 2
                    h_ = hb[i2]
                    hr = f"qa_h{i2}"
                    if c + 1 < T // TB:
                        qa_load(c + 1)
                    for which, dst, dres, scale in ((0, qo[i2], f"qa_q{i2}", 0.125), (1, ko[i2], f"qa_k{i2}", 1.0)):
                        for p in range(8):
                            bank = k % 4
                            k += 1
                            col = which * D + p * 128
                            for fc in range(NFC):
                                P.op("pe", lambda e, fc=fc, col=col, bank=bank, h_=h_: e.matmul(
                                    ps[bank][:, :], W[:, fc, col:col + 128], h_[:, fc, :],
                                    start=(fc == 0), stop=(fc == NFC - 1), skip_group_check=True),
                                    reads=["qa_w", hr], writes=[f"ps{bank}"])
                            if p % 2 == 0:
                                P.op("act", lambda e, p=p, bank=bank, dst=dst, scale=scale: e.activation(
                                    out=dst[:, p, :], in_=ps[bank][:, :], func=AF.Copy, scale=scale),
                                    reads=[f"ps{bank}"], writes=[dres])
                            else:
                                P.op("dve", lambda e, p=p, bank=bank, dst=dst, scale=scale: e.tensor_scalar(
                                    dst[:, p, :], ps[bank][:, :], scale, None, ALU.mult),
                                    reads=[f"ps{bank}"], writes=[dres])
                    P.dma("sp", fcview(qT)[:, :, t0:t0 + TB], qo[i2][:], reads=[f"qa_q{i2}"])
                    P.dma("sp", fcview(kT)[:, :, t0:t0 + TB], ko[i2][:], reads=[f"qa_k{i2}"])
                    v_ = vo[i2]
                    for b in range(NB):
                        for half in range(2):
                            bank = 4 + (k % 4)
                            k += 1
                            for fc in range(NFC):
                                P.op("pe", lambda e, fc=fc, b=b, half=half, bank=bank, h_=h_: e.matmul(
                                    ps[bank][:, :], h_[:, fc, b * 128:(b + 1) * 128],
                                    W[:, fc, 2 * D + half * 512:2 * D + (half + 1) * 512],
                                    start=(fc == 0), stop=(fc == NFC - 1), skip_group_check=True),
                                    reads=["qa_w", hr], writes=[f"ps{bank}"])
                            if half == 0:
                                P.op("act", lambda e, b=b, half=half, bank=bank, v_=v_: e.copy(
                                    v_[:, b, half * 512:(half + 1) * 512], ps[bank][:, :]),
                                    reads=[f"ps{bank}"], writes=[f"qa_v{i2}"])
                            else:
                                P.op("dve", lambda e, b=b, half=half, bank=bank, v_=v_: e.tensor_copy(
                                    v_[:, b, half * 512:(half + 1) * 512], ps[bank][:, :]),
                                    reads=[f"ps{bank}"], writes=[f"qa_v{i2}"])
                    P.dma("sp", vv[t0:t0 + TB, :].rearrange("(b p) n -> p b n", p=128), v_[:], reads=[f"qa_v{i2}"])
                P.flush()

        def phase_att_a():
            with ExitStack() as st:
                triu = T_(st, "aa_triu", [128, 128], BF16)
                comp = T_(st, "aa_comp", [128, 128], BF16)
                mneg = T_(st, "aa_mneg", [128, 128], BF16)
                P.dma("sp", triu[:], I["c_triu"], writes=["aa_triu"])
                P.dma("sp", comp[:], I["c_comp"], writes=["aa_comp"])
                P.dma("sp", mneg[:], I["c_maskneg"], writes=["aa_mneg"])
                load_ident()
                NS = 2
                NBUF = 2
                qs = [[T_(st, f"aa_q{s}_{i}", [64, S], BF16) for i in range(NBUF)] for s in range(NS)]
                ks = [[T_(st, f"aa_k{s}_{i}", [64, S], BF16) for i in range(NBUF)] for s in range(NS)]
                vs = [[T_(st, f"aa_v{s}_{i}", [128, S // 128, HD], BF16) for i in range(NBUF)] for s in range(NS)]
                os_ = [[T_(st, f"aa_o{s}_{i}", [64, S], BF16) for i in range(NBUF)] for s in range(NS)]
                NE, NL, NX, NP_ = 6, 5, 3, 3
                Eb = [T_(st, f"aa_E{i}", [128, 512], F32) for i in range(NE)]
                Lb = [T_(st, f"aa_L{i}", [128, 512], BF16) for i in range(NL)]
                Xb = [T_(st, f"aa_X{i}", [128, 512], F32) for i in range(NX)]
                Pb = [T_(st, f"aa_P{i}", [128, 512], BF16) for i in range(NP_)]
                tiles = []
                jobs = []
                for sq in range(nseq):
                    for hp in range(H // NS):
                        jobs.append([(sq, hp * NS + s) for s in range(NS)])
                tix = [0]

                def mk_tile(s, bi, sq, h, g, kb, first_in_group, last_in_group):
                    ti = tix[0]
                    tix[0] += 1
                    q_, k_, v_, o_ = qs[s][bi], ks[s][bi], vs[s][bi], os_[s][bi]
                    qr, kr, vr, orr = f"aa_q{s}_{bi}", f"aa_k{s}_{bi}", f"aa_v{s}_{bi}", f"aa_o{s}_{bi}"
                    i_d = kb - 4 * g
                    c0 = 128 * i_d if i_d >= 0 else 0
                    diag = i_d >= 0
                    sb = ti % 2
                    cc = 2 + s
                    ob = 4 + s
                    E, L, X, Pt = Eb[ti % NE], Lb[ti % NL], Xb[ti % NX], Pb[ti % NP_]
                    Er, Lr, Xr, Pr = f"aa_E{ti % NE}", f"aa_L{ti % NL}", f"aa_X{ti % NX}", f"aa_P{ti % NP_}"
                    q0 = g * 512

                    def s0():
                        P.op("pe", lambda e: e.matmul(ps[sb][:, c0:512], k_[:, kb * 128:(kb + 1) * 128],
                                                      q_[:, q0 + c0:q0 + 512], start=True, stop=not diag,
                                                      skip_group_check=True),
                             reads=[qr, kr], writes=[f"ps{sb}"])
                        if diag:
                            P.op("pe", lambda e: e.matmul(ps[sb][:, c0:c0 + 128], ident[:], mneg[:],
                                                          start=False, stop=True, skip_group_check=True),
                                 reads=["ident", "aa_mneg"], writes=[f"ps{sb}"])

                    def s1():
                        P.op("act", lambda e: e.activation(out=E[:, c0:512], in_=ps[sb][:, c0:512], func=AF.Exp),
                             reads=[f"ps{sb}"], writes=[Er])

                    def s2():
                        P.op("act", lambda e: e.activation(out=L[:, c0:512], in_=E[:, c0:512], func=AF.Ln, bias=1.0),
                             reads=[Er], writes=[Lr])

                    def s3():
                        P.op("pe", lambda e: e.matmul(ps[cc][:, c0:512], triu[:], L[:, c0:512],
                                                      start=first_in_group, stop=False, skip_group_check=True),
                             reads=[Lr, "aa_triu"], writes=[f"ps{cc}"])

                    def s4():
                        P.op("act", lambda e: e.activation(out=X[:, c0:512], in_=ps[cc][:, c0:512], func=AF.Exp, scale=-1.0),
                             reads=[f"ps{cc}"], writes=[Xr])

                    def s5():
                        P.op("pe", lambda e: e.matmul(ps[cc][:, c0:512], comp[:], L[:, c0:512],
                                                      start=False, stop=last_in_group, skip_group_check=True),
                             reads=[Lr, "aa_comp"], writes=[f"ps{cc}"])
                        P.op("dve", lambda e: e.tensor_tensor(out=Pt[:, c0:512], in0=E[:, c0:512], in1=X[:, c0:512], op=ALU.mult),
                             reads=[Er, Xr], writes=[Pr])

                    def s6():
                        P.op("pe", lambda e: e.matmul(ps[ob][0:64, c0:512], v_[:, kb, :], Pt[:, c0:512],
                                                      start=first_in_group, stop=last_in_group, skip_group_check=True),
                             reads=[Pr, vr], writes=[f"ps{ob}"])
                        if last_in_group:
                            P.op("dve", lambda e: e.tensor_copy(o_[:, q0:q0 + 512], ps[ob][0:64, :]),
                                 reads=[f"ps{ob}"], writes=[orr])
                            if g == S // 512 - 1:
                                tb = sq * S
                                P.dma("sp", oT[h * HD:(h + 1) * HD, tb:tb + S], o_[:], reads=[orr])

                    return [s0, s1, s2, s3, s4, s5, s6]

                def loads(ji):
                    bi = ji % NBUF
                    for s in range(NS):
                        sq, h = jobs[ji][s]
                        tb = sq * S
                        P.dma("sp", qs[s][bi][:], qT[h * HD:(h + 1) * HD, tb:tb + S], writes=[f"aa_q{s}_{bi}"])
                        P.dma("sp", ks[s][bi][:], kT[h * HD:(h + 1) * HD, tb:tb + S], writes=[f"aa_k{s}_{bi}"])
                        P.dma("sp", vs[s][bi][:], vv[tb:tb + S, h * HD:(h + 1) * HD].rearrange("(b p) d -> p b d", p=128),
                              writes=[f"aa_v{s}_{bi}"])

                NST = 7
                pending = []

                def emit_iteration(new_tile):
                    pending.insert(0, new_tile)
                    for sidx in reversed(range(len(pending))):
                        tl = pending[sidx]
                        if tl is not None and sidx < NST:
                            tl[sidx]()
                    if len(pending) >= NST:
                        pending.pop()

                loads(0)
                for ji in range(len(jobs)):
                    bi = ji % NBUF
                    nit = 0
                    for g in range(S // 512):
                        kbs = list(range(4 * g + 3, -1, -1))
                        for n_, kb in enumerate(kbs):
                            for s in range(NS):
                                sq, h = jobs[ji][s]
                                emit_iteration(mk_tile(s, bi, sq, h, g, kb, n_ == 0, n_ == len(kbs) - 1))
                                nit += 1
                                if nit == NST + 1 and ji + 1 < len(jobs):
                                    loads(ji + 1)
                for _ in range(NST):
                    emit_iteration(None)
                P.flush()

        def phase_mlp(L):
            with ExitStack() as st:
                wo_src = I["w_o_a"][L] if L < 2 else I["w_o_b"][L - 2]
                Wo = load_w(st, "ml_wo", wo_src, NFC, D)
                Wg = load_w(st, "ml_wg", I["w_gate"][L], NFC, FF)
                Wu = load_w(st, "ml_wu", I["w_up"][L], NFC, FF)
                Wd = load_w(st, "ml_wd", I["w_down"][L], NJ, D)
                load_ident()

                def gain(name, row):
                    g = T_(st, name, [128, D], F32)
                    P.dma("sp", g[:], bcast_row(row), writes=[name])
                    P.op("dve", lambda e: e.tensor_scalar(g[:], g[:], 32.0, None, ALU.mult), reads=[name], writes=[name])
                    return g

                g_ffn = gain("ml_gf", I["norm_ffn"][L])
                if L < 3:
                    g_next = [("ml_gn", gain("ml_gn", I["norm_mix"][L + 1]), hnT)]
                    if L == 1:
                        g_next.insert(0, ("ml_gk", gain("ml_gk", I["norm_kv"]), hkvT))
                else:
                    g_next = [("ml_gn", gain("ml_gn", I["norm_final"]), None)]
                NB = 2
                TB = NB * 128
                n1 = NormT(st, "m1", NB, 1)
                n2 = NormT(st, "m2", NB, 1 if L < 3 else 0)
                hc = [T_(st, f"ml_h{i}", [128, NB, D], F32) for i in range(2)]
                oc = [T_(st, f"ml_o{i}", [128, NFC, TB], BF16) for i in range(2)]
                hT2 = [T_(st, f"ml_hT2_{i}", [128, NFC, TB], BF16) for i in range(2)]
                hTn = T_(st, "ml_hTn", [128, NFC, TB], BF16) if L < 3 else None
                sg = [T_(st, f"ml_sg{i}", [128, TB], BF16 if L == 1 else F32) for i in range(2)]
                at = [T_(st, f"ml_at{i}", [128, TB], BF16) for i in range(3)]
                yo = T_(st, "ml_y", [128, NB, D], F32) if L == 3 else None
                hsrc = I["x"] if L == 0 else hbuf
                nch = T // TB
                kk = [0]
                jc = [0]
                pend = [None]

                def hres(i):
                    return [f"ml_h{i}_{b}" for b in range(NB)]

                def stage_A_load(c):
                    i = c % 2
                    t0 = c * TB
                    P.dma("sp", hc[i][:], hsrc[t0:t0 + TB, :].rearrange("(b p) d -> p b d", p=128), writes=hres(i))
                    P.dma("sp", oc[i][:], fcview(oT)[:, :, t0:t0 + TB], writes=[f"ml_o{i}"])

                def stage_A(c):
                    i = c % 2
                    h_, o_ = hc[i], oc[i]
                    orr = f"ml_o{i}"
                    for b in range(NB):
                        for half in range(2):
                            bank = 2 + (b * 2 + half) % 2
                            for fc in range(NFC):
                                P.op("pe", lambda e, fc=fc, b=b, half=half, bank=bank, o_=o_: e.matmul(
                                    ps[bank][:, :], o_[:, fc, b * 128:(b + 1) * 128], Wo[:, fc, half * 512:(half + 1) * 512],
                                    start=(fc == 0), stop=(fc == NFC - 1), skip_group_check=True),
                                    reads=[orr, "ml_wo"], writes=[f"ps{bank}"])
                            P.op("dve", lambda e, b=b, half=half, bank=bank, h_=h_: e.tensor_tensor(
                                out=h_[:, b, half * 512:(half + 1) * 512], in0=h_[:, b, half * 512:(half + 1) * 512],
                                in1=ps[bank][:, :], op=ALU.add),
                                reads=[f"ps{bank}", f"ml_h{i}_{b}"], writes=[f"ml_h{i}_{b}"])

                def stage_A_norm(c):
                    i = c % 2
                    h_ = hc[i]
                    for b in range(NB):
                        n1.stats(b, h_[:, b, :], f"ml_h{i}_{b}")
                    n1.finish()
                    for b in range(NB):
                        n1.scale(b, h_[:, b, :], f"ml_h{i}_{b}", 0, g_ffn, "ml_gf")

                def stage_T1(c):
                    i = c % 2
                    for fc in range(NFC):
                        n1.transpose_group(0, fc, hT2[i], f"ml_hT2_{i}")

                def emit_down(a_, ar, j):
                    for b in range(NB):
                        for half in range(2):
                            dbank = 4 + b * 2 + half
                            P.op("pe", lambda e, b=b, half=half, dbank=dbank, a_=a_, j=j: e.matmul(
                                ps[dbank][:, :], a_[:, b * 128:(b + 1) * 128], Wd[:, j, half * 512:(half + 1) * 512],
                                start=(j == 0), stop=(j == NJ - 1), skip_group_check=True),
                                reads=[ar, "ml_wd"], writes=[f"ps{dbank}"])

                def stage_F(c, j0, j1, hook=None):
                    i = c % 2
                    hT = hT2[i]
                    hTr = f"ml_hT2_{i}"
                    for j in range(j0, j1):
                        if hook is not None:
                            hook(j)
                        bank = kk[0] % 2
                        kk[0] += 1
                        a_ = at[jc[0] % 3]
                        ar = f"ml_at{jc[0] % 3}"
                        s_ = sg[jc[0] % 2]
                        sr = f"ml_sg{jc[0] % 2}"
                        jc[0] += 1
                        for gi, Wx in enumerate((Wg, Wu)):
                            for fc in range(NFC):
                                P.op("pe", lambda e, fc=fc, gi=gi, Wx=Wx, bank=bank, j=j, hT=hT: e.matmul(
                                    ps[bank][:, gi * TB:(gi + 1) * TB], Wx[:, fc, j * 128:(j + 1) * 128], hT[:, fc, :],
                                    start=(fc == 0), stop=(fc == NFC - 1), skip_group_check=True),
                                    reads=[hTr, "ml_wg" if gi == 0 else "ml_wu"], writes=[f"ps{bank}"])
                        P.op("act", lambda e, bank=bank, s_=s_: e.activation(out=s_[:], in_=ps[bank][:, 0:TB], func=AF.Silu),
                             reads=[f"ps{bank}"], writes=[sr])
                        P.op("dve", lambda e, bank=bank, s_=s_, a_=a_: e.tensor_tensor(
                            out=a_[:], in0=s_[:], in1=ps[bank][:, TB:2 * TB], op=ALU.mult),
                            reads=[f"ps{bank}", sr], writes=[ar])
                        pend.append((a_, ar, j))
                        if len(pend) > 3:
                            emit_down(*pend.pop(1))
                    if j1 == NJ:
                        while len(pend) > 1:
                            emit_down(*pend.pop(1))

                def norm_out(gname, gt, dst, transposes_now):
                    pass

                def stage_R(c):
                    i = c % 2
                    t0 = c * TB
                    h_ = hc[i]
                    for b in range(NB):
                        for half in range(2):
                            dbank = 4 + b * 2 + half
                            P.op("dve", lambda e, b=b, half=half, dbank=dbank, h_=h_: e.tensor_tensor(
                                out=h_[:, b, half * 512:(half + 1) * 512], in0=h_[:, b, half * 512:(half + 1) * 512],
                                in1=ps[dbank][:, :], op=ALU.add),
                                reads=[f"ps{dbank}", f"ml_h{i}_{b}"], writes=[f"ml_h{i}_{b}"])
                    for b in range(NB):
                        n2.stats(b, h_[:, b, :], f"ml_h{i}_{b}")
                    n2.finish()
                    if L < 3:
                        P.dma("sp", hbuf[t0:t0 + TB, :].rearrange("(b p) d -> p b d", p=128), h_[:], reads=hres(i))
                        for gi, (gname, gt, dst) in enumerate(g_next):
                            for b in range(NB):
                                n2.scale(b, h_[:, b, :], f"ml_h{i}_{b}", 0, gt, gname)
                            if gi < len(g_next) - 1:
                                for fc in range(NFC):
                                    n2.transpose_group(0, fc, hTn, "ml_hTn")
                                P.dma("sp", fcview(dst)[:, :, t0:t0 + TB], hTn[:], reads=["ml_hTn"])
                    else:
                        gt = g_next[0][1]
                        rstd = n2.rstd
                        for b in range(NB):
                            P.op("dve", lambda e, b=b, h_=h_: e.scalar_tensor_tensor(
                                out=yo[:, b, :], in0=h_[:, b, :], scalar=rstd[:, b:b + 1], in1=gt[:],
                                op0=ALU.mult, op1=ALU.mult),
                                reads=[f"ml_h{i}_{b}", f"m2_rstd{b}", "ml_gn"], writes=[f"ml_y{b}"])
                        P.dma("sp", y[t0:t0 + TB, :].rearrange("(b p) d -> p b d", p=128), yo[:],
                              reads=[f"ml_y{b}" for b in range(NB)])

                def stage_T2_group(c, fc):
                    if L == 3:
                        return
                    n2.transpose_group(0, fc, hTn, "ml_hTn")
                    if fc == NFC - 1:
                        t0 = c * TB
                        dst = g_next[-1][2]
                        P.dma("sp", fcview(dst)[:, :, t0:t0 + TB], hTn[:], reads=["ml_hTn"])

                stage_A_load(0)
                stage_A(0)
                stage_A_norm(0)
                stage_T1(0)
                if nch > 1:
                    stage_A_load(1)
                for c in range(nch):
                    def hook(j, c=c):
                        if j == 4 and c + 1 < nch:
                            stage_A(c + 1)
                        if j == 7 and c + 1 < nch:
                            stage_A_norm(c + 1)
                        if 6 <= j < 14 and c > 0:
                            stage_T2_group(c - 1, j - 6)
                        if 14 <= j < 22 and c + 1 < nch:
                            n1.transpose_group(0, j - 14, hT2[(c + 1) % 2], f"ml_hT2_{(c + 1) % 2}")
                    stage_F(c, 0, NJ, hook)
                    stage_R(c)
                    if c + 2 < nch:
                        stage_A_load(c + 2)
                for fc in range(NFC):
                    stage_T2_group(nch - 1, fc)
                P.flush(final=(L == 3))

        def phase_qkv_b(L):
            j = L - 2
            with ExitStack() as st:
                Wq = load_w(st, "qb_wq", I["w_q_b"][j], NFC, D)
                Wqs = load_w_swapped(st, "qb_wqs", I["w_q_b"][j], 0)
                do_kv = (L == 2)
                if do_kv:
                    Wk = load_w(st, "qb_wk", I["w_kv"], NFC, D, 0)
                    Wks = load_w_swapped(st, "qb_wks", I["w_kv"], 0)
                    Wv = load_w(st, "qb_wv", I["w_kv"], NFC, D, D)
                cosb = T_(st, "qb_cos", [128, S], F32)
                sinb = T_(st, "qb_sin", [128, S], F32)
                P.dma("sp", cosb[:], I["c_cos"], writes=["qb_cos"])
                P.dma("sp", sinb[:], I["c_sin"], writes=["qb_sin"])
                NB = 4
                TB = NB * 128
                hb = [T_(st, f"qb_h{i}", [128, NFC, TB], BF16) for i in range(2)]
                kb_ = [T_(st, f"qb_hk{i}", [128, NFC, TB], BF16) for i in range(2)] if do_kv else None
                qo = T_(st, "qb_q", [128, 8, TB], BF16)
                ko = T_(st, "qb_k", [128, 8, TB], BF16) if do_kv else None
                vo = T_(st, "qb_v", [128, NB, D], BF16) if do_kv else None
                t1 = [T_(st, f"qb_t1_{i}", [128, TB], F32) for i in range(2)]
                t2 = [T_(st, f"qb_t2_{i}", [128, TB], F32) for i in range(2)]
                k = 0

                def qb_load(c):
                    P.dma("sp", hb[c % 2][:], fcview(hnT)[:, :, c * TB:(c + 1) * TB], writes=[f"qb_h{c % 2}"])
                    if do_kv:
                        P.dma("sp", kb_[c % 2][:], fcview(hkvT)[:, :, c * TB:(c + 1) * TB], writes=[f"qb_hk{c % 2}"])

                for c in range(T // TB):
                    t0 = c * TB
                    p0 = t0 % S
                    i2 = c % 2
                    h_ = hb[i2]
                    hr = f"qb_h{i2}"
                    if c == 0:
                        qb_load(0)
                    if c + 1 < T // TB:
                        qb_load(c + 1)
                    srcs = [(h_, hr, Wq, Wqs, "qb_wq", "qb_wqs", qo, "qb_q", 0.125)]
                    if do_kv:
                        hk_ = kb_[i2]
                        hkr = f"qb_hk{i2}"
                        srcs.append((hk_, hkr, Wk, Wks, "qb_wk", "qb_wks", ko, "qb_k", 1.0))
                    for (a_, ar, W1, W2, w1r, w2r, dst, dres, scale) in srcs:
                        for p in range(8):
                            bA = (k % 2) * 2
                            bB = bA + 1
                            ta, tbb = t1[k % 2], t2[k % 2]
                            tar, tbr = f"qb_t1_{k % 2}", f"qb_t2_{k % 2}"
                            k += 1
                            for (bank, Wx, wr) in ((bA, W1, w1r), (bB, W2, w2r)):
                                for fc in range(NFC):
                                    P.op("pe", lambda e, fc=fc, bank=bank, Wx=Wx, p=p, a_=a_: e.matmul(
                                        ps[bank][:, :], Wx[:, fc, p * 128:(p + 1) * 128], a_[:, fc, :],
                                        start=(fc == 0), stop=(fc == NFC - 1), skip_group_check=True),
                                        reads=[ar, wr], writes=[f"ps{bank}"])
                            P.op("dve", lambda e, bA=bA, ta=ta, scale=scale, p0=p0: e.scalar_tensor_tensor(
                                out=ta[:], in0=ps[bA][:, :], scalar=scale, in1=cosb[:, p0:p0 + TB], op0=ALU.mult, op1=ALU.mult),
                                reads=[f"ps{bA}", "qb_cos"], writes=[tar])
                            P.op("dve", lambda e, bB=bB, tbb=tbb, scale=scale, p0=p0: e.scalar_tensor_tensor(
                                out=tbb[:], in0=ps[bB][:, :], scalar=scale, in1=sinb[:, p0:p0 + TB], op0=ALU.mult, op1=ALU.mult),
                                reads=[f"ps{bB}", "qb_sin"], writes=[tbr])
                            P.op("pool", lambda e, ta=ta, tbb=tbb, dst=dst, p=p: e.tensor_tensor(
                                out=dst[:, p, :], in0=ta[:], in1=tbb[:], op=ALU.add),
                                reads=[tar, tbr], writes=[dres])
                    P.dma("sp", fcview(qT)[:, :, t0:t0 + TB], qo[:], reads=["qb_q"])
                    if do_kv:
                        P.dma("sp", fcview(kT)[:, :, t0:t0 + TB], ko[:], reads=["qb_k"])
                        for b in range(NB):
                            for half in range(2):
                                bank = 4 + (k % 4)
                                k += 1
                                for fc in range(NFC):
                                    P.op("pe", lambda e, fc=fc, b=b, half=half, bank=bank, hk_=hk_: e.matmul(
                                        ps[bank][:, :], hk_[:, fc, b * 128:(b + 1) * 128],
                                        Wv[:, fc, half * 512:(half + 1) * 512],
                                        start=(fc == 0), stop=(fc == NFC - 1), skip_group_check=True),
                                        reads=["qb_wv", hkr], writes=[f"ps{bank}"])
                                P.op("act", lambda e, b=b, half=half, bank=bank: e.copy(
                                    vo[:, b, half * 512:(half + 1) * 512], ps[bank][:, :]),
                                    reads=[f"ps{bank}"], writes=["qb_v"])
                        P.dma("sp", vv[t0:t0 + TB, :].rearrange("(b p) n -> p b n", p=128), vo[:], reads=["qb_v"])
                P.flush()

        def phase_att_b():
            with ExitStack() as st:
                m01 = T_(st, "ab_m01", [128, 512], BF16)
                m01f = T_(st, "ab_m01f", [128, 512], BF16)
                zk = T_(st, "ab_zk", [64, 128], BF16)
                shiftm = T_(st, "ab_shift", [128, 64], F32)
                P.dma("sp", m01[:], I["c_m01"], writes=["ab_m01"])
                P.dma("sp", m01f[:], I["c_m01f"], writes=["ab_m01f"])
                P.dma("sp", shiftm[:], I["c_shift"], writes=["ab_shift"])
                P.op("dve", lambda e: e.memset(zk[:], 0.0), writes=["ab_zk"])
                NS = 2
                qs = [T_(st, f"ab_q{s}", [64, S], BF16) for s in range(NS)]
                ks = [T_(st, f"ab_k{s}", [64, S], BF16) for s in range(NS)]
                vs = [[[T_(st, f"ab_v{s}_{i}_{r}", [128, S // 128, 128], BF16) for r in range(3)] for i in range(2)]
                      for s in range(NS)]
                for s in range(NS):
                    for i in range(2):
                        for r in range(3):
                            P.op("pool", lambda e, vt=vs[s][i][r]: e.memset(vt[:, :, 64:128], 1.0),
                                 writes=[f"ab_v{s}_{i}_{r}"])
                nd = [T_(st, f"ab_nd{s}", [128, S], F32) for s in range(NS)]
                ot = [T_(st, f"ab_ot{s}", [64, S], BF16) for s in range(NS)]
                NPB = 4
                Pb = [T_(st, f"ab_P{i}", [128, 512], BF16) for i in range(NPB)]
                jobs = []
                for sq in range(nseq):
                    for hp in range(H // NS):
                        jobs.append([(sq, hp * NS + s) for s in range(NS)])
                tix = [0]

                def mk_tile(s, vb, sq, h, ri, r, c, kb0, first_tile, last_tile):
                    ti = tix[0]
                    tix[0] += 1
                    q_, k_ = qs[s], ks[s]
                    v_ = vs[s][vb][ri]
                    qr, kr, vr = f"ab_q{s}", f"ab_k{s}", f"ab_v{s}_{vb}_{ri}"
                    n = S // r
                    nbc = n // 128
                    last_pair = (kb0 + 2 == nbc)
                    sbank = ti % 4
                    nbank = 4 + ti % 4
                    Pt = Pb[ti % NPB]
                    Pr = f"ab_P{ti % NPB}"
                    ndr = f"ab_nd{s}"
                    n2 = 128 if last_pair else 256
                    ncols = 256 + n2

                    def tsl(nb, cnt):
                        a = c + r * 128 * nb
                        return slice(a, a + (cnt - 1) * r + 1, r)

                    def s0():
                        P.op("pe", lambda e: e.matmul(ps[sbank][:, 0:256], k_[:, tsl(kb0, 128)], q_[:, tsl(kb0, 256)],
                                                      start=True, stop=True, skip_group_check=True),
                             reads=[qr, kr], writes=[f"ps{sbank}"])
                        P.op("pe", lambda e: e.matmul(ps[sbank][:, 256:256 + n2], k_[:, tsl(kb0 + 1, 128)],
                                                      q_[:, tsl(kb0 + 1, n2)],
                                                      start=True, stop=True, skip_group_check=True),
                             reads=[qr, kr], writes=[f"ps{sbank}"])

                    def s1():
                        P.op("act", lambda e: e.activation(out=Pt[:, 0:ncols], in_=ps[sbank][:, 0:ncols], func=AF.Exp),
                             reads=[f"ps{sbank}"], writes=[Pr])

                    def s2():
                        P.op("pool", lambda e: e.tensor_tensor(out=Pt[:, 0:ncols], in0=Pt[:, 0:ncols], in1=m01[:, 0:ncols],
                                                               op=ALU.mult),
                             reads=[Pr, "ab_m01"], writes=[Pr])

                    def s3():
                        P.op("pe", lambda e: e.matmul(ps[nbank][:, 0:256], v_[:, c * nbc + kb0, :], Pt[:, 0:256],
                                                      start=True, stop=False, skip_group_check=True),
                             reads=[Pr, vr], writes=[f"ps{nbank}"])
                        P.op("pe", lambda e: e.matmul(ps[nbank][:, 128:128 + n2], v_[:, c * nbc + kb0 + 1, :],
                                                      Pt[:, 256:256 + n2],
                                                      start=False, stop=True, skip_group_check=True),
                             reads=[Pr, vr], writes=[f"ps{nbank}"])

                    def s4():
                        nq = 128 + n2
                        sl = tsl(kb0, nq)
                        if first_tile:
                            P.op("pool", lambda e: e.memset(nd[s][:, :], 0.0), writes=[ndr])
                        P.op("dve", lambda e: e.tensor_tensor(out=nd[s][:, sl], in0=nd[s][:, sl],
                                                              in1=ps[nbank][:, 0:nq], op=ALU.add),
                             reads=[f"ps{nbank}", ndr], writes=[ndr])
                        if last_tile:
                            orr = f"ab_ot{s}"
                            P.op("act", lambda e: e.activation(out=nd[s][64:128, :], in_=nd[s][64:128, :], func=AF.Ln),
                                 reads=[ndr], writes=[ndr])
                            P.op("act", lambda e: e.activation(out=nd[s][64:128, :], in_=nd[s][64:128, :], func=AF.Exp, scale=-1.0),
                                 reads=[ndr], writes=[ndr])
                            for ch in range(S // 512):
                                fb = 4 + (ti + 1 + ch) % 4
                                P.op("pe", lambda e, ch=ch, fb=fb: e.matmul(
                                    ps[fb][0:64, :], shiftm[:, :], nd[s][:, ch * 512:(ch + 1) * 512],
                                    start=True, stop=True, skip_group_check=True),
                                    reads=[ndr, "ab_shift"], writes=[f"ps{fb}"])
                                P.op("dve", lambda e, ch=ch, fb=fb: e.tensor_tensor(
                                    out=ot[s][:, ch * 512:(ch + 1) * 512], in0=nd[s][0:64, ch * 512:(ch + 1) * 512],
                                    in1=ps[fb][0:64, :], op=ALU.mult),
                                    reads=[f"ps{fb}", ndr], writes=[orr])
                            tb = sq * S
                            P.dma("sp", oT[h * HD:(h + 1) * HD, tb:tb + S], ot[s][:], reads=[orr])

                    return [s0, s1, s2, s3, s4]

                def loads_qk(ji):
                    for s in range(NS):
                        sq, h = jobs[ji][s]
                        tb = sq * S
                        P.dma("sp", qs[s][:], qT[h * HD:(h + 1) * HD, tb:tb + S], writes=[f"ab_q{s}"])
                        P.dma("sp", ks[s][:], kT[h * HD:(h + 1) * HD, tb:tb + S], writes=[f"ab_k{s}"])

                def loads_v(ji):
                    vb = ji % 2
                    for s in range(NS):
                        sq, h = jobs[ji][s]
                        tb = sq * S
                        for ri, r in enumerate(BRANCHES):
                            nbc = S // r // 128
                            vsrc = vv[tb:tb + S, h * HD:(h + 1) * HD]
                            vt = vs[s][vb][ri]
                            for c in range(r):
                                src = vsrc.rearrange("(nb i c) d -> c i nb d", i=128, c=r)[c]
                                P.dma("sp", vt[:, c * nbc:(c + 1) * nbc, 0:64], src, writes=[f"ab_v{s}_{vb}_{ri}"])

                NST = 5
                pending = []

                def emit_iteration(new_tile):
                    pending.insert(0, new_tile)
                    for sidx in reversed(range(len(pending))):
                        tl = pending[sidx]
                        if tl is not None and sidx < NST:
                            tl[sidx]()
                    if len(pending) >= NST:
                        pending.pop()

                loads_v(0)
                for ji in range(len(jobs)):
                    vb = ji % 2
                    nit = 0
                    loads_qk(ji)
                    tl = []
                    for ri, r in enumerate(BRANCHES):
                        nbc = S // r // 128
                        for c in range(r):
                            for nb0 in range(0, nbc, 2):
                                tl.append((ri, r, c, nb0))
                    for n_, (ri, r, c, nb0) in enumerate(tl):
                        for s in range(NS):
                            sq, h = jobs[ji][s]
                            emit_iteration(mk_tile(s, vb, sq, h, ri, r, c, nb0, n_ == 0, n_ == len(tl) - 1))
                            nit += 1
                            if nit == NST + 1 and ji + 1 < len(jobs):
                                loads_v(ji + 1)
                for _ in range(NST):
                    emit_iteration(None)
                P.flush()

        seq = [("pre", phase_pre)]
        for L in range(DEPTH):
            if L < 2:
                seq.append((f"qkv{L}", lambda L=L: phase_qkv_a(L)))
                seq.append((f"att{L}", phase_att_a))
            else:
                seq.append((f"qkv{L}", lambda L=L: phase_qkv_b(L)))
                seq.append((f"att{L}", phase_att_b))
            seq.append((f"mlp{L}", lambda L=L: phase_mlp(L)))
        for name, fn in seq:
            fn()
            if stop_after == name:
                P.flush(final=True)
                break
    return nc


_NC_CACHE = {}


def kernel(**inputs):
    nseq = 2
    if "nc" not in _NC_CACHE:
        _NC_CACHE["nc"] = build_program(nseq=nseq)
    nc = _NC_CACHE["nc"]
    consts = _consts()
    x = np.ascontiguousarray(np.asarray(inputs["x"], dtype=np.float32))
    B = x.shape[0]
    per = B // NCORES
    in_maps = []
    for c in range(NCORES):
        m = {"x": np.ascontiguousarray(x[c * per:(c + 1) * per].reshape(per * S, D))}
        for k in ("norm_mix", "w_qkv_a", "w_o_a", "norm_kv", "w_kv", "w_q_b", "w_o_b", "norm_ffn",
                  "w_gate", "w_up", "w_down", "norm_final"):
            m[k] = np.ascontiguousarray(np.asarray(inputs[k], dtype=np.float32))
        m.update(consts)
        in_maps.append(m)
    res = run_bass_kernel_spmd(nc, in_maps, core_ids=list(range(NCORES)))
    out = np.concatenate([np.asarray(r["y"]).reshape(per, S, D) for r in res.results], axis=0)
    return out.astype(np.float32)
```

```python
import numpy as np
import ml_dtypes
from contextlib import ExitStack
import concourse.bass as bass
import concourse.mybir as mybir
from concourse.bass_utils import run_bass_kernel_spmd

F32 = mybir.dt.float32
BF16 = mybir.dt.bfloat16
AF = mybir.ActivationFunctionType
ALU = mybir.AluOpType

D = 1024
S = 4096
H = 16
HD = 64
FF = 2816
NFC = 8
NJ = 22
DEPTH = 4
NCORES = 8
EPS = 1e-6
NEG = -30000.0
BRANCHES = (1, 4, 16)


class Prog:
    CE = ("pe", "act", "dve", "pool")

    def __init__(self, nc, st):
        self.nc = nc
        self.sem = {e: st.enter_context(nc.semaphore("s_" + e)) for e in self.CE}
        self.cnt = {e: 0 for e in self.CE}
        self.dq = {
            "sp": [st.enter_context(nc.semaphore(f"d_sp{i}")) for i in range(8)],
            "pool": [st.enter_context(nc.semaphore(f"d_pl{i}")) for i in range(4)],
        }
        self.dcnt = {"sp": [0] * 8, "pool": [0] * 4}
        self.dk = {"sp": 0, "pool": 0}
        self.reset()

    def reset(self):
        self.ops = []
        self.last_w = {}
        self.readers = {}

    def op(self, eng, fn, reads=(), writes=(), dma=False):
        i = len(self.ops)
        deps = set()
        for r in reads:
            if r in self.last_w:
                deps.add(self.last_w[r])
        for w in writes:
            if w in self.last_w:
                deps.add(self.last_w[w])
            deps.update(self.readers.get(w, ()))
        for r in reads:
            self.readers.setdefault(r, []).append(i)
        for w in writes:
            self.last_w[w] = i
            self.readers[w] = []
        deps.discard(i)
        self.ops.append(dict(eng=eng, fn=fn, deps=deps, dma=dma))
        return i

    def dma(self, q, out, in_, reads=(), writes=()):
        return self.op(q, lambda e: e.dma_start(out=out, in_=in_), reads, writes, dma=True)

    def flush(self, final=False):
        ops = self.ops
        n = len(ops)
        has_dep = [False] * n
        for o in ops:
            for d in o["deps"]:
                has_dep[d] = True
        last = {}
        for i, o in enumerate(ops):
            if not o["dma"]:
                last[o["eng"]] = i
        for i in last.values():
            has_dep[i] = True
        tok = [None] * n
        pre_wait = [None] * n
        cnt = dict(self.cnt)
        dk = dict(self.dk)
        dcnt = {k: list(v) for k, v in self.dcnt.items()}
        for i, o in enumerate(ops):
            if o["dma"]:
                q = o["eng"]
                R = len(self.dq[q])
                si = dk[q] % R
                dk[q] += 1
                prev = dcnt[q][si]
                dcnt[q][si] += 1
                tok[i] = (self.dq[q][si], 16 * dcnt[q][si])
                if prev > 0:
                    pre_wait[i] = (self.dq[q][si], 16 * prev)
            elif has_dep[i]:
                e = o["eng"]
                cnt[e] += 1
                tok[i] = (self.sem[e], cnt[e])
        start_tokens = [(self.sem[e], self.cnt[e]) for e in self.CE if self.cnt[e] > 0]
        for q in self.dq:
            for s_, c_ in zip(self.dq[q], self.dcnt[q]):
                if c_ > 0:
                    start_tokens.append((s_, 16 * c_))
        end_tokens = []
        for q in self.dq:
            for s_, c_ in zip(self.dq[q], dcnt[q]):
                if c_ > 0:
                    end_tokens.append((s_, 16 * c_))

        with self.nc.Block() as blk:
            for qname, deco in (("pe", blk.tensor), ("act", blk.scalar), ("dve", blk.vector),
                                ("pool", blk.gpsimd), ("sp", blk.sync)):
                def body(e, qname=qname):
                    waited = {}

                    def wait(t):
                        s_, v = t
                        k = id(s_)
                        if waited.get(k, 0) >= v:
                            return
                        waited[k] = v
                        e.wait_ge(s_, v)

                    for t in start_tokens:
                        wait(t)
                    for i, o in enumerate(ops):
                        if o["eng"] != qname:
                            continue
                        for d in sorted(o["deps"]):
                            od = ops[d]
                            if qname == "pe" and od["eng"] == "pe" and not od["dma"]:
                                continue
                            wait(tok[d])
                        if pre_wait[i] is not None:
                            wait(pre_wait[i])
                        ins = o["fn"](e)
                        if tok[i] is not None:
                            ins.then_inc(tok[i][0], 16 if o["dma"] else 1)
                    if final and qname == "sp":
                        for t in end_tokens:
                            wait(t)
                        for ce in self.CE:
                            if cnt[ce] > 0:
                                wait((self.sem[ce], cnt[ce]))
                deco(body)
        self.cnt = cnt
        self.dk = dk
        self.dcnt = dcnt
        self.reset()


def _consts():
    bf = ml_dtypes.bfloat16
    j = np.arange(128)[:, None]
    t = np.arange(128)[None, :]
    ident = np.eye(128, dtype=np.float32).astype(bf)
    triu = (j >= t).astype(np.float32).astype(bf)
    comp = (j < t).astype(np.float32).astype(bf)
    maskneg = np.where(j < t, 0.0, NEG).astype(np.float32).astype(bf)
    prev = (j >= t).astype(np.float32)
    cur = (j <= t).astype(np.float32)
    m01 = np.concatenate([cur, prev, cur, prev], axis=1).astype(bf)
    m01f = np.concatenate([np.zeros_like(prev), cur, prev, cur], axis=1).astype(bf)
    inv_freq = (10000.0 ** (-np.arange(0, HD, 2, dtype=np.float32) / HD)).astype(np.float32)
    ang = (np.arange(S, dtype=np.float32)[:, None] * inv_freq[None, :]).astype(np.float32)
    cos = np.cos(ang).astype(np.float32).T
    sin = np.sin(ang).astype(np.float32).T
    cos64 = np.concatenate([cos, cos], axis=0)
    sin64 = np.concatenate([-sin, sin], axis=0)
    cos2 = np.ascontiguousarray(np.concatenate([cos64, cos64], axis=0))
    sin2 = np.ascontiguousarray(np.concatenate([sin64, sin64], axis=0))
    shift = np.zeros((128, 64), np.float32)
    shift[np.arange(64) + 64, np.arange(64)] = 1.0
    return dict(c_ident=ident, c_triu=triu, c_comp=comp, c_maskneg=maskneg, c_m01=m01, c_m01f=m01f,
                c_cos=cos2, c_sin=sin2, c_shift=shift)


_CONST_SPECS = dict(c_ident=([128, 128], BF16), c_triu=([128, 128], BF16), c_comp=([128, 128], BF16),
                    c_maskneg=([128, 128], BF16), c_m01=([128, 512], BF16), c_m01f=([128, 512], BF16),
                    c_cos=([128, S], F32), c_sin=([128, S], F32), c_shift=([128, 64], F32))


def build_program(nseq=2, debug=False, stop_after=None):
    T = nseq * S
    nc = bass.Bass("TRN2", target_bir_lowering=False)
    I = {}

    def din(name, shape, dt=F32):
        I[name] = nc.dram_tensor(name, list(shape), dt, kind="ExternalInput").ap()

    din("x", [T, D])
    din("norm_mix", [DEPTH, D])
    din("w_qkv_a", [2, D, 3 * D])
    din("w_o_a", [2, D, D])
    din("norm_kv", [D])
    din("w_kv", [D, 2 * D])
    din("w_q_b", [2, D, D])
    din("w_o_b", [2, D, D])
    din("norm_ffn", [DEPTH, D])
    din("w_gate", [DEPTH, D, FF])
    din("w_up", [DEPTH, D, FF])
    din("w_down", [DEPTH, FF, D])
    din("norm_final", [D])
    for k, (shp, dt) in _CONST_SPECS.items():
        din(k, shp, dt)
    y = nc.dram_tensor("y", [T, D], F32, kind="ExternalOutput").ap()
    skind = "ExternalOutput" if debug else "Internal"

    def scratch(name, shape, dt):
        return nc.dram_tensor(name, list(shape), dt, kind=skind).ap()

    hbuf = scratch("s_h", [T, D], F32)
    hnT = scratch("s_hnT", [D, T], BF16)
    hkvT = scratch("s_hkvT", [D, T], BF16)
    qT = scratch("s_qT", [D, T], BF16)
    kT = scratch("s_kT", [D, T], BF16)
    vv = scratch("s_v", [T, D], BF16)
    oT = scratch("s_oT", [D, T], BF16)

    gst = ExitStack()
    with gst, nc.allow_low_precision("bf16 matmul operands, fp32 accumulate"):
        P = Prog(nc, gst)
        ps = [gst.enter_context(nc.psum_tensor(f"ps{i}", [128, 512], F32)) for i in range(8)]
        psb = [p.bitcast(BF16) for p in ps]
        ident = gst.enter_context(nc.sbuf_tensor("ident", [128, 128], BF16))
        rstd_eps = 1024.0 * EPS

        uid = [0]

        def T_(st, name, shape, dt):
            uid[0] += 1
            return st.enter_context(nc.sbuf_tensor(f"{name}_u{uid[0]}", list(shape), dt))

        first = [True]

        def load_ident():
            if first[0]:
                P.dma("sp", ident[:], I["c_ident"], writes=["ident"])
                first[0] = False

        def bcast_row(ap1d):
            return ap1d.partition_broadcast(128)

        class NormT:
            def __init__(self, st, tag, nb, ngain):
                self.tag = tag
                self.nb = nb
                self.hn = [T_(st, f"{tag}_hn{g}", [128, nb, D], BF16) for g in range(ngain)]
                self.junk = T_(st, f"{tag}_junk", [128, D], BF16) if ngain == 0 else None
                self.ss = T_(st, f"{tag}_ss", [128, 8], F32)
                self.rstd = T_(st, f"{tag}_rstd", [128, 8], F32)
                self.k = 0

            def stats(self, b, src, src_res):
                tag = self.tag
                ss, rstd = self.ss, self.rstd
                if self.junk is not None:
                    junk, jres = self.junk[:], f"{tag}_junk"
                else:
                    junk, jres = self.hn[0][:, b, :], f"{tag}_hn0_{b}"
                P.op("dve", lambda e: e.memset(ss[:, b:b + 1], 0.0), writes=[f"{tag}_ss{b}"])
                P.op("act", lambda e: e.activation(out=junk, in_=src, func=AF.Square, accum_out=ss[:, b:b + 1]),
                     reads=[src_res], writes=[jres, f"{tag}_ss{b}"])

            def finish(self):
                tag, nb = self.tag, self.nb
                ss, rstd = self.ss, self.rstd
                P.op("act", lambda e: e.activation(out=rstd[:, 0:nb], in_=ss[:, 0:nb], func=AF.Ln, bias=rstd_eps),
                     reads=[f"{tag}_ss{b}" for b in range(nb)], writes=[f"{tag}_rstd{b}" for b in range(nb)])
                P.op("act", lambda e: e.activation(out=rstd[:, 0:nb], in_=rstd[:, 0:nb], func=AF.Exp, scale=-0.5),
                     reads=[f"{tag}_rstd{b}" for b in range(nb)], writes=[f"{tag}_rstd{b}" for b in range(nb)])

            def scale(self, b, src, src_res, gi, gain, gain_res):
                tag = self.tag
                hn, rstd = self.hn[gi], self.rstd
                P.op("dve", lambda e: e.scalar_tensor_tensor(out=hn[:, b, :], in0=src, scalar=rstd[:, b:b + 1],
                                                              in1=gain[:], op0=ALU.mult, op1=ALU.mult),
                     reads=[src_res, f"{tag}_rstd{b}", gain_res], writes=[f"{tag}_hn{gi}_{b}"])

            def transpose_group(self, gi, fc, dst, dst_res, banks=(2, 3)):
                tag, nb = self.tag, self.nb
                hn = self.hn[gi]
                bank = banks[self.k % len(banks)]
                self.k += 1
                sres = f"ps{bank}"
                for b in range(nb):
                    P.op("pe", lambda e, b=b: e.transpose(
                        psb[bank][:, b * 128:(b + 1) * 128], hn[:, b, fc * 128:(fc + 1) * 128], ident[:]),
                        reads=[f"{tag}_hn{gi}_{b}", "ident"], writes=[sres])
                if fc % 2 == 0:
                    P.op("act", lambda e: e.copy(dst[:, fc, :], psb[bank][:, 0:nb * 128]),
                         reads=[sres], writes=[dst_res])
                else:
                    P.op("dve", lambda e: e.tensor_copy(dst[:, fc, :], psb[bank][:, 0:nb * 128]),
                         reads=[sres], writes=[dst_res])

            def transpose(self, gi, dst, dst_res, banks):
                tag, nb = self.tag, self.nb
                hn = self.hn[gi]
                for fc in range(NFC):
                    bank = banks[self.k % len(banks)]
                    self.k += 1
                    for b in range(nb):
                        P.op("pe", lambda e, b=b, fc=fc, bank=bank: e.transpose(
                            psb[bank][:, b * 128:(b + 1) * 128], hn[:, b, fc * 128:(fc + 1) * 128], ident[:]),
                            reads=[f"{tag}_hn{gi}_{b}", "ident"], writes=[f"ps{bank}"])
                    eng = "act" if fc % 2 == 0 else "dve"
                    if eng == "act":
                        P.op("act", lambda e, fc=fc, bank=bank: e.copy(dst[:, fc, :], psb[bank][:, 0:nb * 128]),
                             reads=[f"ps{bank}"], writes=[dst_res])
                    else:
                        P.op("dve", lambda e, fc=fc, bank=bank: e.tensor_copy(dst[:, fc, :], psb[bank][:, 0:nb * 128]),
                             reads=[f"ps{bank}"], writes=[dst_res])

        def fcview(ap2d):
            return ap2d.rearrange("(c p) n -> p c n", p=128)

        def load_w(st, name, src, nchunk, ncol, c0=0):
            w = T_(st, name, [128, nchunk, ncol], BF16)
            for c in range(nchunk):
                P.dma("pool", w[:, c, :], src[c * 128:(c + 1) * 128, c0:c0 + ncol], writes=[name])
            return w

        def load_w_swapped(st, name, src, c0):
            w = T_(st, name, [128, NFC, D], BF16)
            for c in range(NFC):
                sv = src[c * 128:(c + 1) * 128, c0:c0 + D].rearrange("k (h two d) -> k h two d", two=2, d=32)
                dv = w[:, c, :].rearrange("p (h two d) -> p h two d", two=2, d=32)
                P.dma("pool", dv[:, :, 0, :], sv[:, :, 1, :], writes=[name])
                P.dma("pool", dv[:, :, 1, :], sv[:, :, 0, :], writes=[name])
            return w

        def phase_pre():
            with ExitStack() as st:
                load_ident()
                g = T_(st, "pre_g", [128, D], F32)
                P.dma("sp", g[:], bcast_row(I["norm_mix"][0]), writes=["pre_g"])
                P.op("dve", lambda e: e.tensor_scalar(g[:], g[:], 32.0, None, ALU.mult), reads=["pre_g"], writes=["pre_g"])
                NB = 4
                nt = NormT(st, "pre", NB, 1)
                xb = [T_(st, f"pre_x{i}", [128, NB, D], F32) for i in range(2)]
                ob = [T_(st, f"pre_o{i}", [128, NFC, NB * 128], BF16) for i in range(2)]
                nchp = T // (NB * 128)

                def pre_load(c):
                    t0 = c * NB * 128
                    P.dma("sp", xb[c % 2][:], I["x"][t0:t0 + NB * 128, :].rearrange("(b p) d -> p b d", p=128),
                          writes=[f"pre_x{c % 2}"])

                pre_load(0)
                for c in range(nchp):
                    t0 = c * NB * 128
                    x_, o_ = xb[c % 2], ob[c % 2]
                    xr, orr = f"pre_x{c % 2}", f"pre_o{c % 2}"
                    if c + 1 < nchp:
                        pre_load(c + 1)
                    for b in range(NB):
                        nt.stats(b, x_[:, b, :], xr)
                    nt.finish()
                    for b in range(NB):
                        nt.scale(b, x_[:, b, :], xr, 0, g, "pre_g")
                    nt.transpose(0, o_, orr, [0, 1])
                    P.dma("sp", fcview(hnT)[:, :, t0:t0 + NB * 128], o_[:], reads=[orr])
                P.flush()

        def phase_qkv_a(L):
            with ExitStack() as st:
                W = load_w(st, "qa_w", I["w_qkv_a"][L], NFC, 3 * D)
                NB = 4
                TB = NB * 128
                hb = [T_(st, f"qa_h{i}", [128, NFC, TB], BF16) for i in range(2)]
                qo = [T_(st, f"qa_q{i}", [128, 8, TB], BF16) for i in range(2)]
                ko = [T_(st, f"qa_k{i}", [128, 8, TB], BF16) for i in range(2)]
                vo = [T_(st, f"qa_v{i}", [128, NB, D], BF16) for i in range(2)]
                k = 0

                def qa_load(c):
                    P.dma("sp", hb[c % 2][:], fcview(hnT)[:, :, c * TB:(c + 1) * TB], writes=[f"qa_h{c % 2}"])

                qa_load(0)
                for c in range(T // TB):
                    t0 = c * TB
                    i2 = c % 2
                    h_ = hb[i2]
                    hr = f"qa_h{i2}"
                    if c + 1 < T // TB:
                        qa_load(c + 1)
                    for which, dst, dres, scale in ((0, qo[i2], f"qa_q{i2}", 0.125), (1, ko[i2], f"qa_k{i2}", 1.0)):
                        for p in range(8):
                            bank = k % 4
                            k += 1
                            col = which * D + p * 128
                            for fc in range(NFC):
                                P.op("pe", lambda e, fc=fc, col=col, bank=bank, h_=h_: e.matmul(
                                    ps[bank][:, :], W[:, fc, col:col + 128], h_[:, fc, :],
                                    start=(fc == 0), stop=(fc == NFC - 1), skip_group_check=True),
                                    reads=["qa_w", hr], writes=[f"ps{bank}"])
                            if p % 2 == 0:
                                P.op("act", lambda e, p=p, bank=bank, dst=dst, scale=scale: e.activation(
                                    out=dst[:, p, :], in_=ps[bank][:, :], func=AF.Copy, scale=scale),
                                    reads=[f"ps{bank}"], writes=[dres])
                            else:
                                P.op("dve", lambda e, p=p, bank=bank, dst=dst, scale=scale: e.tensor_scalar(
                                    dst[:, p, :], ps[bank][:, :], scale, None, ALU.mult),
                                    reads=[f"ps{bank}"], writes=[dres])
                    P.dma("sp", fcview(qT)[:, :, t0:t0 + TB], qo[i2][:], reads=[f"qa_q{i2}"])
                    P.dma("sp", fcview(kT)[:, :, t0:t0 + TB], ko[i2][:], reads=[f"qa_k{i2}"])
                    v_ = vo[i2]
                    for b in range(NB):
                        for half in range(2):
                            bank = 4 + (k % 4)
                            k += 1
                            for fc in range(NFC):
                                P.op("pe", lambda e, fc=fc, b=b, half=half, bank=bank, h_=h_: e.matmul(
                                    ps[bank][:, :], h_[:, fc, b * 128:(b + 1) * 128],
                                    W[:, fc, 2 * D + half * 512:2 * D + (half + 1) * 512],
                                    start=(fc == 0), stop=(fc == NFC - 1), skip_group_check=True),
                                    reads=["qa_w", hr], writes=[f"ps{bank}"])
                            if half == 0:
                                P.op("act", lambda e, b=b, half=half, bank=bank, v_=v_: e.copy(
                                    v_[:, b, half * 512:(half + 1) * 512], ps[bank][:, :]),
                                    reads=[f"ps{bank}"], writes=[f"qa_v{i2}"])
                            else:
                                P.op("dve", lambda e, b=b, half=half, bank=bank, v_=v_: e.tensor_copy(
                                    v_[:, b, half * 512:(half + 1) * 512], ps[bank][:, :]),
                                    reads=[f"ps{bank}"], writes=[f"qa_v{i2}"])
                    P.dma("sp", vv[t0:t0 + TB, :].rearrange("(b p) n -> p b n", p=128), v_[:], reads=[f"qa_v{i2}"])
                P.flush()

        def phase_att_a():
            with ExitStack() as st:
                triu = T_(st, "aa_triu", [128, 128], BF16)
                comp = T_(st, "aa_comp", [128, 128], BF16)
                mneg = T_(st, "aa_mneg", [128, 128], BF16)
                P.dma("sp", triu[:], I["c_triu"], writes=["aa_triu"])
                P.dma("sp", comp[:], I["c_comp"], writes=["aa_comp"])
                P.dma("sp", mneg[:], I["c_maskneg"], writes=["aa_mneg"])
                load_ident()
                NS = 2
                NBUF = 2
                qs = [[T_(st, f"aa_q{s}_{i}", [64, S], BF16) for i in range(NBUF)] for s in range(NS)]
                ks = [[T_(st, f"aa_k{s}_{i}", [64, S], BF16) for i in range(NBUF)] for s in range(NS)]
                vs = [[T_(st, f"aa_v{s}_{i}", [128, S // 128, HD], BF16) for i in range(NBUF)] for s in range(NS)]
                os_ = [[T_(st, f"aa_o{s}_{i}", [64, S], BF16) for i in range(NBUF)] for s in range(NS)]
                NE, NL, NX, NP_ = 6, 5, 3, 3
                Eb = [T_(st, f"aa_E{i}", [128, 512], F32) for i in range(NE)]
                Lb = [T_(st, f"aa_L{i}", [128, 512], BF16) for i in range(NL)]
                Xb = [T_(st, f"aa_X{i}", [128, 512], F32) for i in range(NX)]
                Pb = [T_(st, f"aa_P{i}", [128, 512], BF16) for i in range(NP_)]
                tiles = []
                jobs = []
                for sq in range(nseq):
                    for hp in range(H // NS):
                        jobs.append([(sq, hp * NS + s) for s in range(NS)])
                tix = [0]

                def mk_tile(s, bi, sq, h, g, kb, first_in_group, last_in_group):
                    ti = tix[0]
                    tix[0] += 1
                    q_, k_, v_, o_ = qs[s][bi], ks[s][bi], vs[s][bi], os_[s][bi]
                    qr, kr, vr, orr = f"aa_q{s}_{bi}", f"aa_k{s}_{bi}", f"aa_v{s}_{bi}", f"aa_o{s}_{bi}"
                    i_d = kb - 4 * g
                    c0 = 128 * i_d if i_d >= 0 else 0
                    diag = i_d >= 0
                    sb = ti % 2
                    cc = 2 + s
                    ob = 4 + s
                    E, L, X, Pt = Eb[ti % NE], Lb[ti % NL], Xb[ti % NX], Pb[ti % NP_]
                    Er, Lr, Xr, Pr = f"aa_E{ti % NE}", f"aa_L{ti % NL}", f"aa_X{ti % NX}", f"aa_P{ti % NP_}"
                    q0 = g * 512

                    def s0():
                        P.op("pe", lambda e: e.matmul(ps[sb][:, c0:512], k_[:, kb * 128:(kb + 1) * 128],
                                                      q_[:, q0 + c0:q0 + 512], start=True, stop=not diag,
                                                      skip_group_check=True),
                             reads=[qr, kr], writes=[f"ps{sb}"])
                        if diag:
                            P.op("pe", lambda e: e.matmul(ps[sb][:, c0:c0 + 128], ident[:], mneg[:],
                                                          start=False, stop=True, skip_group_check=True),
                                 reads=["ident", "aa_mneg"], writes=[f"ps{sb}"])

                    def s1():
                        P.op("act", lambda e: e.activation(out=E[:, c0:512], in_=ps[sb][:, c0:512], func=AF.Exp),
                             reads=[f"ps{sb}"], writes=[Er])

                    def s2():
                        P.op("act", lambda e: e.activation(out=L[:, c0:512], in_=E[:, c0:512], func=AF.Ln, bias=1.0),
                             reads=[Er], writes=[Lr])

                    def s3():
                        P.op("pe", lambda e: e.matmul(ps[cc][:, c0:512], triu[:], L[:, c0:512],
                                                      start=first_in_group, stop=False, skip_group_check=True),
                             reads=[Lr, "aa_triu"], writes=[f"ps{cc}"])

                    def s4():
                        P.op("act", lambda e: e.activation(out=X[:, c0:512], in_=ps[cc][:, c0:512], func=AF.Exp, scale=-1.0),
                             reads=[f"ps{cc}"], writes=[Xr])

                    def s5():
                        P.op("pe", lambda e: e.matmul(ps[cc][:, c0:512], comp[:], L[:, c0:512],
                                                      start=False, stop=last_in_group, skip_group_check=True),
                             reads=[Lr, "aa_comp"], writes=[f"ps{cc}"])
                        P.op("dve", lambda e: e.tensor_tensor(out=Pt[:, c0:512], in0=E[:, c0:512], in1=X[:, c0:512], op=ALU.mult),
                             reads=[Er, Xr], writes=[Pr])

                    def s6():
                        P.op("pe", lambda e: e.matmul(ps[ob][0:64, c0:512], v_[:, kb, :], Pt[:, c0:512],
                                                      start=first_in_group, stop=last_in_group, skip_group_check=True),
                             reads=[Pr, vr], writes=[f"ps{ob}"])
                        if last_in_group:
                            P.op("dve", lambda e: e.tensor_copy(o_[:, q0:q0 + 512], ps[ob][0:64, :]),
                                 reads=[f"ps{ob}"], writes=[orr])
                            if g == S // 512 - 1:
                                tb = sq * S
                                P.dma("sp", oT[h * HD:(h + 1) * HD, tb:tb + S], o_[:], reads=[orr])

                    return [s0, s1, s2, s3, s4, s5, s6]

                def loads(ji):
                    bi = ji % NBUF
                    for s in range(NS):
                        sq, h = jobs[ji][s]
                        tb = sq * S
                        P.dma("sp", qs[s][bi][:], qT[h * HD:(h + 1) * HD, tb:tb + S], writes=[f"aa_q{s}_{bi}"])
                        P.dma("sp", ks[s][bi][:], kT[h * HD:(h + 1) * HD, tb:tb + S], writes=[f"aa_k{s}_{bi}"])
                        P.dma("sp", vs[s][bi][:], vv[tb:tb + S, h * HD:(h + 1) * HD].rearrange("(b p) d -> p b d", p=128),
                              writes=[f"aa_v{s}_{bi}"])

                NST = 7
                pending = []

                def emit_iteration(new_tile):
                    pending.insert(0, new_tile)
                    for sidx in reversed(range(len(pending))):
                        tl = pending[sidx]
                        if tl is not None and sidx < NST:
                            tl[sidx]()
                    if len(pending) >= NST:
                        pending.pop()

                loads(0)
                for ji in range(len(jobs)):
                    bi = ji % NBUF
                    nit = 0
                    for g in range(S // 512):
                        kbs = list(range(4 * g + 3, -1, -1))
                        for n_, kb in enumerate(kbs):
                            for s in range(NS):
                                sq, h = jobs[ji][s]
                                emit_iteration(mk_tile(s, bi, sq, h, g, kb, n_ == 0, n_ == len(kbs) - 1))
                                nit += 1
                                if nit == NST + 1 and ji + 1 < len(jobs):
                                    loads(ji + 1)
                for _ in range(NST):
                    emit_iteration(None)
                P.flush()

        def phase_mlp(L):
            with ExitStack() as st:
                wo_src = I["w_o_a"][L] if L < 2 else I["w_o_b"][L - 2]
                Wo = load_w(st, "ml_wo", wo_src, NFC, D)
                Wg = load_w(st, "ml_wg", I["w_gate"][L], NFC, FF)
                Wu = load_w(st, "ml_wu", I["w_up"][L], NFC, FF)
                Wd = load_w(st, "ml_wd", I["w_down"][L], NJ, D)
                load_ident()

                def gain(name, row):
                    g = T_(st, name, [128, D], F32)
                    P.dma("sp", g[:], bcast_row(row), writes=[name])
                    P.op("dve", lambda e: e.tensor_scalar(g[:], g[:], 32.0, None, ALU.mult), reads=[name], writes=[name])
                    return g

                g_ffn = gain("ml_gf", I["norm_ffn"][L])
                if L < 3:
                    g_next = [("ml_gn", gain("ml_gn", I["norm_mix"][L + 1]), hnT)]
                    if L == 1:
                        g_next.insert(0, ("ml_gk", gain("ml_gk", I["norm_kv"]), hkvT))
                else:
                    g_next = [("ml_gn", gain("ml_gn", I["norm_final"]), None)]
                NB = 2
                TB = NB * 128
                n1 = NormT(st, "m1", NB, 1)
                n2 = NormT(st, "m2", NB, 1 if L < 3 else 0)
                hc = [T_(st, f"ml_h{i}", [128, NB, D], F32) for i in range(2)]
                oc = [T_(st, f"ml_o{i}", [128, NFC, TB], BF16) for i in range(2)]
                hT2 = [T_(st, f"ml_hT2_{i}", [128, NFC, TB], BF16) for i in range(2)]
                hTn = T_(st, "ml_hTn", [128, NFC, TB], BF16) if L < 3 else None
                sg = [T_(st, f"ml_sg{i}", [128, TB], BF16 if L == 1 else F32) for i in range(2)]
                at = [T_(st, f"ml_at{i}", [128, TB], BF16) for i in range(3)]
                yo = T_(st, "ml_y", [128, NB, D], F32) if L == 3 else None
                hsrc = I["x"] if L == 0 else hbuf
                nch = T // TB
                kk = [0]
                jc = [0]
                pend = [None]

                def hres(i):
                    return [f"ml_h{i}_{b}" for b in range(NB)]

                def stage_A_load(c):
                    i = c % 2
                    t0 = c * TB
                    P.dma("sp", hc[i][:], hsrc[t0:t0 + TB, :].rearrange("(b p) d -> p b d", p=128), writes=hres(i))
                    P.dma("sp", oc[i][:], fcview(oT)[:, :, t0:t0 + TB], writes=[f"ml_o{i}"])

                def stage_A(c):
                    i = c % 2
                    h_, o_ = hc[i], oc[i]
                    orr = f"ml_o{i}"
                    for b in range(NB):
                        for half in range(2):
                            bank = 2 + (b * 2 + half) % 2
                            for fc in range(NFC):
                                P.op("pe", lambda e, fc=fc, b=b, half=half, bank=bank, o_=o_: e.matmul(
                                    ps[bank][:, :], o_[:, fc, b * 128:(b + 1) * 128], Wo[:, fc, half * 512:(half + 1) * 512],
                                    start=(fc == 0), stop=(fc == NFC - 1), skip_group_check=True),
                                    reads=[orr, "ml_wo"], writes=[f"ps{bank}"])
                            P.op("dve", lambda e, b=b, half=half, bank=bank, h_=h_: e.tensor_tensor(
                                out=h_[:, b, half * 512:(half + 1) * 512], in0=h_[:, b, half * 512:(half + 1) * 512],
                                in1=ps[bank][:, :], op=ALU.add),
                                reads=[f"ps{bank}", f"ml_h{i}_{b}"], writes=[f"ml_h{i}_{b}"])

                def stage_A_norm(c, part=None):
                    i = c % 2
                    h_ = hc[i]
                    for b in range(NB):
                        if part is None or part == b:
                            n1.stats(b, h_[:, b, :], f"ml_h{i}_{b}")
                    if part is None or part == 2:
                        n1.finish()
                    if part is None or part == 3:
                        for b in range(NB):
                            n1.scale(b, h_[:, b, :], f"ml_h{i}_{b}", 0, g_ffn, "ml_gf")

                def stage_T1(c):
                    i = c % 2
                    for fc in range(NFC):
                        n1.transpose_group(0, fc, hT2[i], f"ml_hT2_{i}")

                def emit_down(a_, ar, j):
                    for b in range(NB):
                        for half in range(2):
                            dbank = 4 + b * 2 + half
                            P.op("pe", lambda e, b=b, half=half, dbank=dbank, a_=a_, j=j: e.matmul(
                                ps[dbank][:, :], a_[:, b * 128:(b + 1) * 128], Wd[:, j, half * 512:(half + 1) * 512],
                                start=(j == 0), stop=(j == NJ - 1), skip_group_check=True),
                                reads=[ar, "ml_wd"], writes=[f"ps{dbank}"])

                def stage_F(c, j0, j1, hook=None):
                    i = c % 2
                    hT = hT2[i]
                    hTr = f"ml_hT2_{i}"
                    for j in range(j0, j1):
                        if hook is not None:
                            hook(j)
                        bank = kk[0] % 2
                        kk[0] += 1
                        a_ = at[jc[0] % 3]
                        ar = f"ml_at{jc[0] % 3}"
                        s_ = sg[jc[0] % 2]
                        sr = f"ml_sg{jc[0] % 2}"
                        jc[0] += 1
                        for gi, Wx in enumerate((Wg, Wu)):
                            for fc in range(NFC):
                                P.op("pe", lambda e, fc=fc, gi=gi, Wx=Wx, bank=bank, j=j, hT=hT: e.matmul(
                                    ps[bank][:, gi * TB:(gi + 1) * TB], Wx[:, fc, j * 128:(j + 1) * 128], hT[:, fc, :],
                                    start=(fc == 0), stop=(fc == NFC - 1), skip_group_check=True),
                                    reads=[hTr, "ml_wg" if gi == 0 else "ml_wu"], writes=[f"ps{bank}"])
                        P.op("act", lambda e, bank=bank, s_=s_: e.activation(out=s_[:], in_=ps[bank][:, 0:TB], func=AF.Silu),
                             reads=[f"ps{bank}"], writes=[sr])
                        P.op("dve", lambda e, bank=bank, s_=s_, a_=a_: e.tensor_tensor(
                            out=a_[:], in0=s_[:], in1=ps[bank][:, TB:2 * TB], op=ALU.mult),
                            reads=[f"ps{bank}", sr], writes=[ar])
                        pend.append((a_, ar, j))
                        if len(pend) > 3:
                            emit_down(*pend.pop(1))
                    if j1 == NJ:
                        while len(pend) > 1:
                            emit_down(*pend.pop(1))

                def norm_out(gname, gt, dst, transposes_now):
                    pass

                def stage_R(c, part=None):
                    i = c % 2
                    t0 = c * TB
                    h_ = hc[i]
                    if part is None or part == 0:
                        for b in range(NB):
                            for half in range(2):
                                dbank = 4 + b * 2 + half
                                P.op("dve", lambda e, b=b, half=half, dbank=dbank, h_=h_: e.tensor_tensor(
                                    out=h_[:, b, half * 512:(half + 1) * 512], in0=h_[:, b, half * 512:(half + 1) * 512],
                                    in1=ps[dbank][:, :], op=ALU.add),
                                    reads=[f"ps{dbank}", f"ml_h{i}_{b}"], writes=[f"ml_h{i}_{b}"])
                        if L < 3:
                            P.dma("sp", hbuf[t0:t0 + TB, :].rearrange("(b p) d -> p b d", p=128), h_[:], reads=hres(i))
                    for b in range(NB):
                        if part is None or part == b:
                            n2.stats(b, h_[:, b, :], f"ml_h{i}_{b}")
                    if part is None or part == 2:
                        n2.finish()
                    if part is None or part == 3:
                        if L < 3:
                            for gi, (gname, gt, dst) in enumerate(g_next):
                                for b in range(NB):
                                    n2.scale(b, h_[:, b, :], f"ml_h{i}_{b}", 0, gt, gname)
                                if gi < len(g_next) - 1:
                                    for fc in range(NFC):
                                        n2.transpose_group(0, fc, hTn, "ml_hTn")
                                    P.dma("sp", fcview(dst)[:, :, t0:t0 + TB], hTn[:], reads=["ml_hTn"])
                        else:
                            gt = g_next[0][1]
                            rstd = n2.rstd
                            for b in range(NB):
                                P.op("dve", lambda e, b=b, h_=h_: e.scalar_tensor_tensor(
                                    out=yo[:, b, :], in0=h_[:, b, :], scalar=rstd[:, b:b + 1], in1=gt[:],
                                    op0=ALU.mult, op1=ALU.mult),
                                    reads=[f"ml_h{i}_{b}", f"m2_rstd{b}", "ml_gn"], writes=[f"ml_y{b}"])
                            P.dma("sp", y[t0:t0 + TB, :].rearrange("(b p) d -> p b d", p=128), yo[:],
                                  reads=[f"ml_y{b}" for b in range(NB)])

                def stage_T2_group(c, fc):
                    if L == 3:
                        return
                    n2.transpose_group(0, fc, hTn, "ml_hTn")
                    if fc == NFC - 1:
                        t0 = c * TB
                        dst = g_next[-1][2]
                        P.dma("sp", fcview(dst)[:, :, t0:t0 + TB], hTn[:], reads=["ml_hTn"])

                stage_A_load(0)
                stage_A(0)
                stage_A_norm(0)
                stage_T1(0)
                if nch > 1:
                    stage_A_load(1)
                for c in range(nch):
                    def hook(j, c=c):
                        if 1 <= j <= 3 and c > 0:
                            stage_R(c - 1, j)
                            if j == 3 and c + 1 < nch:
                                stage_A_load(c + 1)
                        if j == 7 and c + 1 < nch:
                            stage_A(c + 1)
                        if 9 <= j <= 12 and c + 1 < nch:
                            stage_A_norm(c + 1, j - 9)
                        if 6 <= j < 14 and c > 0:
                            stage_T2_group(c - 1, j - 6)
                        if 14 <= j < 22 and c + 1 < nch:
                            n1.transpose_group(0, j - 14, hT2[(c + 1) % 2], f"ml_hT2_{(c + 1) % 2}")
                    stage_F(c, 0, NJ, hook)
                    stage_R(c, 0)
                for part in (1, 2, 3):
                    stage_R(nch - 1, part)
                for fc in range(NFC):
                    stage_T2_group(nch - 1, fc)
                P.flush(final=(L == 3))

        def phase_qkv_b(L):
            j = L - 2
            with ExitStack() as st:
                Wq = load_w(st, "qb_wq", I["w_q_b"][j], NFC, D)
                Wqs = load_w_swapped(st, "qb_wqs", I["w_q_b"][j], 0)
                do_kv = (L == 2)
                if do_kv:
                    Wk = load_w(st, "qb_wk", I["w_kv"], NFC, D, 0)
                    Wks = load_w_swapped(st, "qb_wks", I["w_kv"], 0)
                    Wv = load_w(st, "qb_wv", I["w_kv"], NFC, D, D)
                cosb = T_(st, "qb_cos", [128, S], F32)
                sinb = T_(st, "qb_sin", [128, S], F32)
                P.dma("sp", cosb[:], I["c_cos"], writes=["qb_cos"])
                P.dma("sp", sinb[:], I["c_sin"], writes=["qb_sin"])
                NB = 4
                TB = NB * 128
                hb = [T_(st, f"qb_h{i}", [128, NFC, TB], BF16) for i in range(2)]
                kb_ = [T_(st, f"qb_hk{i}", [128, NFC, TB], BF16) for i in range(2)] if do_kv else None
                qo = T_(st, "qb_q", [128, 8, TB], BF16)
                ko = T_(st, "qb_k", [128, 8, TB], BF16) if do_kv else None
                vo = T_(st, "qb_v", [128, NB, D], BF16) if do_kv else None
                t1 = [T_(st, f"qb_t1_{i}", [128, TB], F32) for i in range(2)]
                t2 = [T_(st, f"qb_t2_{i}", [128, TB], F32) for i in range(2)]
                k = 0

                def qb_load(c):
                    P.dma("sp", hb[c % 2][:], fcview(hnT)[:, :, c * TB:(c + 1) * TB], writes=[f"qb_h{c % 2}"])
                    if do_kv:
                        P.dma("sp", kb_[c % 2][:], fcview(hkvT)[:, :, c * TB:(c + 1) * TB], writes=[f"qb_hk{c % 2}"])

                for c in range(T // TB):
                    t0 = c * TB
                    p0 = t0 % S
                    i2 = c % 2
                    h_ = hb[i2]
                    hr = f"qb_h{i2}"
                    if c == 0:
                        qb_load(0)
                    if c + 1 < T // TB:
                        qb_load(c + 1)
                    srcs = [(h_, hr, Wq, Wqs, "qb_wq", "qb_wqs", qo, "qb_q", 0.125)]
                    if do_kv:
                        hk_ = kb_[i2]
                        hkr = f"qb_hk{i2}"
                        srcs.append((hk_, hkr, Wk, Wks, "qb_wk", "qb_wks", ko, "qb_k", 1.0))
                    for (a_, ar, W1, W2, w1r, w2r, dst, dres, scale) in srcs:
                        for p in range(8):
                            bA = (k % 2) * 2
                            bB = bA + 1
                            ta, tbb = t1[k % 2], t2[k % 2]
                            tar, tbr = f"qb_t1_{k % 2}", f"qb_t2_{k % 2}"
                            k += 1
                            for (bank, Wx, wr) in ((bA, W1, w1r), (bB, W2, w2r)):
                                for fc in range(NFC):
                                    P.op("pe", lambda e, fc=fc, bank=bank, Wx=Wx, p=p, a_=a_: e.matmul(
                                        ps[bank][:, :], Wx[:, fc, p * 128:(p + 1) * 128], a_[:, fc, :],
                                        start=(fc == 0), stop=(fc == NFC - 1), skip_group_check=True),
                                        reads=[ar, wr], writes=[f"ps{bank}"])
                            P.op("dve", lambda e, bA=bA, ta=ta, scale=scale, p0=p0: e.scalar_tensor_tensor(
                                out=ta[:], in0=ps[bA][:, :], scalar=scale, in1=cosb[:, p0:p0 + TB], op0=ALU.mult, op1=ALU.mult),
                                reads=[f"ps{bA}", "qb_cos"], writes=[tar])
                            P.op("dve", lambda e, bB=bB, tbb=tbb, scale=scale, p0=p0: e.scalar_tensor_tensor(
                                out=tbb[:], in0=ps[bB][:, :], scalar=scale, in1=sinb[:, p0:p0 + TB], op0=ALU.mult, op1=ALU.mult),
                                reads=[f"ps{bB}", "qb_sin"], writes=[tbr])
                            P.op("pool", lambda e, ta=ta, tbb=tbb, dst=dst, p=p: e.tensor_tensor(
                                out=dst[:, p, :], in0=ta[:], in1=tbb[:], op=ALU.add),
                                reads=[tar, tbr], writes=[dres])
                    P.dma("sp", fcview(qT)[:, :, t0:t0 + TB], qo[:], reads=["qb_q"])
                    if do_kv:
                        P.dma("sp", fcview(kT)[:, :, t0:t0 + TB], ko[:], reads=["qb_k"])
                        for b in range(NB):
                            for half in range(2):
                                bank = 4 + (k % 4)
                                k += 1
                                for fc in range(NFC):
                                    P.op("pe", lambda e, fc=fc, b=b, half=half, bank=bank, hk_=hk_: e.matmul(
                                        ps[bank][:, :], hk_[:, fc, b * 128:(b + 1) * 128],
                                        Wv[:, fc, half * 512:(half + 1) * 512],
                                        start=(fc == 0), stop=(fc == NFC - 1), skip_group_check=True),
                                        reads=["qb_wv", hkr], writes=[f"ps{bank}"])
                                P.op("act", lambda e, b=b, half=half, bank=bank: e.copy(
                                    vo[:, b, half * 512:(half + 1) * 512], ps[bank][:, :]),
                                    reads=[f"ps{bank}"], writes=["qb_v"])
                        P.dma("sp", vv[t0:t0 + TB, :].rearrange("(b p) n -> p b n", p=128), vo[:], reads=["qb_v"])
                P.flush()

        def phase_att_b():
            with ExitStack() as st:
                m01 = T_(st, "ab_m01", [128, 512], BF16)
                m01f = T_(st, "ab_m01f", [128, 512], BF16)
                zk = T_(st, "ab_zk", [64, 128], BF16)
                shiftm = T_(st, "ab_shift", [128, 64], F32)
                P.dma("sp", m01[:], I["c_m01"], writes=["ab_m01"])
                P.dma("sp", m01f[:], I["c_m01f"], writes=["ab_m01f"])
                P.dma("sp", shiftm[:], I["c_shift"], writes=["ab_shift"])
                P.op("dve", lambda e: e.memset(zk[:], 0.0), writes=["ab_zk"])
                NS = 2
                qs = [T_(st, f"ab_q{s}", [64, S], BF16) for s in range(NS)]
                ks = [T_(st, f"ab_k{s}", [64, S], BF16) for s in range(NS)]
                vs = [[[T_(st, f"ab_v{s}_{i}_{r}", [128, S // 128, 128], BF16) for r in range(3)] for i in range(2)]
                      for s in range(NS)]
                for s in range(NS):
                    for i in range(2):
                        for r in range(3):
                            P.op("pool", lambda e, vt=vs[s][i][r]: e.memset(vt[:, :, 64:128], 1.0),
                                 writes=[f"ab_v{s}_{i}_{r}"])
                nd = [T_(st, f"ab_nd{s}", [128, S], F32) for s in range(NS)]
                ot = [T_(st, f"ab_ot{s}", [64, S], BF16) for s in range(NS)]
                NPB = 4
                Pb = [T_(st, f"ab_P{i}", [128, 512], BF16) for i in range(NPB)]
                jobs = []
                for sq in range(nseq):
                    for hp in range(H // NS):
                        jobs.append([(sq, hp * NS + s) for s in range(NS)])
                tix = [0]

                def mk_tile(s, vb, sq, h, ri, r, c, kb0, first_tile, last_tile):
                    ti = tix[0]
                    tix[0] += 1
                    q_, k_ = qs[s], ks[s]
                    v_ = vs[s][vb][ri]
                    qr, kr, vr = f"ab_q{s}", f"ab_k{s}", f"ab_v{s}_{vb}_{ri}"
                    n = S // r
                    nbc = n // 128
                    last_pair = (kb0 + 2 == nbc)
                    sbank = ti % 4
                    nbank = 4 + ti % 4
                    Pt = Pb[ti % NPB]
                    Pr = f"ab_P{ti % NPB}"
                    ndr = f"ab_nd{s}"
                    n2 = 128 if last_pair else 256
                    ncols = 256 + n2

                    def tsl(nb, cnt):
                        a = c + r * 128 * nb
                        return slice(a, a + (cnt - 1) * r + 1, r)

                    def s0():
                        P.op("pe", lambda e: e.matmul(ps[sbank][:, 0:256], k_[:, tsl(kb0, 128)], q_[:, tsl(kb0, 256)],
                                                      start=True, stop=True, skip_group_check=True),
                             reads=[qr, kr], writes=[f"ps{sbank}"])
                        P.op("pe", lambda e: e.matmul(ps[sbank][:, 256:256 + n2], k_[:, tsl(kb0 + 1, 128)],
                                                      q_[:, tsl(kb0 + 1, n2)],
                                                      start=True, stop=True, skip_group_check=True),
                             reads=[qr, kr], writes=[f"ps{sbank}"])

                    def s1():
                        P.op("act", lambda e: e.activation(out=Pt[:, 0:ncols], in_=ps[sbank][:, 0:ncols], func=AF.Exp),
                             reads=[f"ps{sbank}"], writes=[Pr])

                    def s2():
                        P.op("pool" if s == 0 else "dve",
                             lambda e: e.tensor_tensor(out=Pt[:, 0:ncols], in0=Pt[:, 0:ncols], in1=m01[:, 0:ncols],
                                                       op=ALU.mult),
                             reads=[Pr, "ab_m01"], writes=[Pr])

                    def s3():
                        P.op("pe", lambda e: e.matmul(ps[nbank][:, 0:256], v_[:, c * nbc + kb0, :], Pt[:, 0:256],
                                                      start=True, stop=False, skip_group_check=True),
                             reads=[Pr, vr], writes=[f"ps{nbank}"])
                        P.op("pe", lambda e: e.matmul(ps[nbank][:, 128:128 + n2], v_[:, c * nbc + kb0 + 1, :],
                                                      Pt[:, 256:256 + n2],
                                                      start=False, stop=True, skip_group_check=True),
                             reads=[Pr, vr], writes=[f"ps{nbank}"])

                    def s4():
                        nq = 128 + n2
                        sl = tsl(kb0, nq)
                        if first_tile:
                            P.op("pool", lambda e: e.memset(nd[s][:, :], 0.0), writes=[ndr])
                        P.op("dve", lambda e: e.tensor_tensor(out=nd[s][:, sl], in0=nd[s][:, sl],
                                                              in1=ps[nbank][:, 0:nq], op=ALU.add),
                             reads=[f"ps{nbank}", ndr], writes=[ndr])
                        if last_tile:
                            orr = f"ab_ot{s}"
                            P.op("act", lambda e: e.activation(out=nd[s][64:128, :], in_=nd[s][64:128, :], func=AF.Ln),
                                 reads=[ndr], writes=[ndr])
                            P.op("act", lambda e: e.activation(out=nd[s][64:128, :], in_=nd[s][64:128, :], func=AF.Exp, scale=-1.0),
                                 reads=[ndr], writes=[ndr])
                            for ch in range(S // 512):
                                fb = 4 + (ti + 1 + ch) % 4
                                P.op("pe", lambda e, ch=ch, fb=fb: e.matmul(
                                    ps[fb][0:64, :], shiftm[:, :], nd[s][:, ch * 512:(ch + 1) * 512],
                                    start=True, stop=True, skip_group_check=True),
                                    reads=[ndr, "ab_shift"], writes=[f"ps{fb}"])
                                P.op("dve", lambda e, ch=ch, fb=fb: e.tensor_tensor(
                                    out=ot[s][:, ch * 512:(ch + 1) * 512], in0=nd[s][0:64, ch * 512:(ch + 1) * 512],
                                    in1=ps[fb][0:64, :], op=ALU.mult),
                                    reads=[f"ps{fb}", ndr], writes=[orr])
                            tb = sq * S
                            P.dma("sp", oT[h * HD:(h + 1) * HD, tb:tb + S], ot[s][:], reads=[orr])

                    return [s0, s1, s2, s3, s4]

                def loads_qk(ji):
                    for s in range(NS):
                        sq, h = jobs[ji][s]
                        tb = sq * S
                        P.dma("sp", qs[s][:], qT[h * HD:(h + 1) * HD, tb:tb + S], writes=[f"ab_q{s}"])
                        P.dma("sp", ks[s][:], kT[h * HD:(h + 1) * HD, tb:tb + S], writes=[f"ab_k{s}"])

                def loads_v(ji):
                    vb = ji % 2
                    for s in range(NS):
                        sq, h = jobs[ji][s]
                        tb = sq * S
                        for ri, r in enumerate(BRANCHES):
                            nbc = S // r // 128
                            vsrc = vv[tb:tb + S, h * HD:(h + 1) * HD]
                            vt = vs[s][vb][ri]
                            for c in range(r):
                                src = vsrc.rearrange("(nb i c) d -> c i nb d", i=128, c=r)[c]
                                P.dma("sp", vt[:, c * nbc:(c + 1) * nbc, 0:64], src, writes=[f"ab_v{s}_{vb}_{ri}"])

                NST = 5
                pending = []

                def emit_iteration(new_tile):
                    pending.insert(0, new_tile)
                    for sidx in reversed(range(len(pending))):
                        tl = pending[sidx]
                        if tl is not None and sidx < NST:
                            tl[sidx]()
                    if len(pending) >= NST:
                        pending.pop()

                loads_v(0)
                for ji in range(len(jobs)):
                    vb = ji % 2
                    nit = 0
                    loads_qk(ji)
                    tl = []
                    for ri, r in enumerate(BRANCHES):
                        nbc = S // r // 128
                        for c in range(r):
                            for nb0 in range(0, nbc, 2):
                                tl.append((ri, r, c, nb0))
                    for n_, (ri, r, c, nb0) in enumerate(tl):
                        for s in range(NS):
                            sq, h = jobs[ji][s]
                            emit_iteration(mk_tile(s, vb, sq, h, ri, r, c, nb0, n_ == 0, n_ == len(tl) - 1))
                            nit += 1
                            if nit == NST + 1 and ji + 1 < len(jobs):
                                loads_v(ji + 1)
                for _ in range(NST):
                    emit_iteration(None)
                P.flush()

        seq = [("pre", phase_pre)]
        for L in range(DEPTH):
            if L < 2:
                seq.append((f"qkv{L}", lambda L=L: phase_qkv_a(L)))
                seq.append((f"att{L}", phase_att_a))
            else:
                seq.append((f"qkv{L}", lambda L=L: phase_qkv_b(L)))
                seq.append((f"att{L}", phase_att_b))
            seq.append((f"mlp{L}", lambda L=L: phase_mlp(L)))
        for name, fn in seq:
            fn()
            if stop_after == name:
                P.flush(final=True)
                break
    return nc


_NC_CACHE = {}


def kernel(**inputs):
    nseq = 2
    if "nc" not in _NC_CACHE:
        _NC_CACHE["nc"] = build_program(nseq=nseq)
    nc = _NC_CACHE["nc"]
    consts = _consts()
    x = np.ascontiguousarray(np.asarray(inputs["x"], dtype=np.float32))
    B = x.shape[0]
    per = B // NCORES
    in_maps = []
    for c in range(NCORES):
        m = {"x": np.ascontiguousarray(x[c * per:(c + 1) * per].reshape(per * S, D))}
        for k in ("norm_mix", "w_qkv_a", "w_o_a", "norm_kv", "w_kv", "w_q_b", "w_o_b", "norm_ffn",
                  "w_gate", "w_up", "w_down", "norm_final"):
            m[k] = np.ascontiguousarray(np.asarray(inputs[k], dtype=np.float32))
        m.update(consts)
        in_maps.append(m)
    res = run_bass_kernel_spmd(nc, in_maps, core_ids=list(range(NCORES)))
    out = np.concatenate([np.asarray(r["y"]).reshape(per, S, D) for r in res.results], axis=0)
    return out.astype(np.float32)
```

```python
import numpy as np
import ml_dtypes
from contextlib import ExitStack
import concourse.bass as bass
import concourse.mybir as mybir
from concourse.bass_utils import run_bass_kernel_spmd

F32 = mybir.dt.float32
BF16 = mybir.dt.bfloat16
AF = mybir.ActivationFunctionType
ALU = mybir.AluOpType

D = 1024
S = 4096
H = 16
HD = 64
FF = 2816
NFC = 8
NJ = 22
DEPTH = 4
NCORES = 8
EPS = 1e-6
NEG = -30000.0
BRANCHES = (1, 4, 16)


class Prog:
    CE = ("pe", "act", "dve", "pool")

    def __init__(self, nc, st):
        self.nc = nc
        self.sem = {e: st.enter_context(nc.semaphore("s_" + e)) for e in self.CE}
        self.cnt = {e: 0 for e in self.CE}
        self.dq = {
            "sp": [st.enter_context(nc.semaphore(f"d_sp{i}")) for i in range(8)],
            "pool": [st.enter_context(nc.semaphore(f"d_pl{i}")) for i in range(4)],
        }
        self.dcnt = {"sp": [0] * 8, "pool": [0] * 4}
        self.dk = {"sp": 0, "pool": 0}
        self.reset()

    def reset(self):
        self.ops = []
        self.last_w = {}
        self.readers = {}

    def op(self, eng, fn, reads=(), writes=(), dma=False):
        i = len(self.ops)
        deps = set()
        for r in reads:
            if r in self.last_w:
                deps.add(self.last_w[r])
        for w in writes:
            if w in self.last_w:
                deps.add(self.last_w[w])
            deps.update(self.readers.get(w, ()))
        for r in reads:
            self.readers.setdefault(r, []).append(i)
        for w in writes:
            self.last_w[w] = i
            self.readers[w] = []
        deps.discard(i)
        self.ops.append(dict(eng=eng, fn=fn, deps=deps, dma=dma))
        return i

    def dma(self, q, out, in_, reads=(), writes=()):
        return self.op(q, lambda e: e.dma_start(out=out, in_=in_), reads, writes, dma=True)

    def flush(self, final=False):
        ops = self.ops
        n = len(ops)
        has_dep = [False] * n
        for o in ops:
            for d in o["deps"]:
                has_dep[d] = True
        last = {}
        for i, o in enumerate(ops):
            if not o["dma"]:
                last[o["eng"]] = i
        for i in last.values():
            has_dep[i] = True
        tok = [None] * n
        pre_wait = [None] * n
        cnt = dict(self.cnt)
        dk = dict(self.dk)
        dcnt = {k: list(v) for k, v in self.dcnt.items()}
        for i, o in enumerate(ops):
            if o["dma"]:
                q = o["eng"]
                R = len(self.dq[q])
                si = dk[q] % R
                dk[q] += 1
                prev = dcnt[q][si]
                dcnt[q][si] += 1
                tok[i] = (self.dq[q][si], 16 * dcnt[q][si])
                if prev > 0:
                    pre_wait[i] = (self.dq[q][si], 16 * prev)
            elif has_dep[i]:
                e = o["eng"]
                cnt[e] += 1
                tok[i] = (self.sem[e], cnt[e])
        start_tokens = [(self.sem[e], self.cnt[e]) for e in self.CE if self.cnt[e] > 0]
        for q in self.dq:
            for s_, c_ in zip(self.dq[q], self.dcnt[q]):
                if c_ > 0:
                    start_tokens.append((s_, 16 * c_))
        end_tokens = []
        for q in self.dq:
            for s_, c_ in zip(self.dq[q], dcnt[q]):
                if c_ > 0:
                    end_tokens.append((s_, 16 * c_))

        with self.nc.Block() as blk:
            for qname, deco in (("pe", blk.tensor), ("act", blk.scalar), ("dve", blk.vector),
                                ("pool", blk.gpsimd), ("sp", blk.sync)):
                def body(e, qname=qname):
                    waited = {}

                    def wait(t):
                        s_, v = t
                        k = id(s_)
                        if waited.get(k, 0) >= v:
                            return
                        waited[k] = v
                        e.wait_ge(s_, v)

                    for t in start_tokens:
                        wait(t)
                    for i, o in enumerate(ops):
                        if o["eng"] != qname:
                            continue
                        for d in sorted(o["deps"]):
                            od = ops[d]
                            if qname == "pe" and od["eng"] == "pe" and not od["dma"]:
                                continue
                            wait(tok[d])
                        if pre_wait[i] is not None:
                            wait(pre_wait[i])
                        ins = o["fn"](e)
                        if tok[i] is not None:
                            ins.then_inc(tok[i][0], 16 if o["dma"] else 1)
                    if final and qname == "sp":
                        for t in end_tokens:
                            wait(t)
                        for ce in self.CE:
                            if cnt[ce] > 0:
                                wait((self.sem[ce], cnt[ce]))
                deco(body)
        self.cnt = cnt
        self.dk = dk
        self.dcnt = dcnt
        self.reset()


def _consts():
    bf = ml_dtypes.bfloat16
    j = np.arange(128)[:, None]
    t = np.arange(128)[None, :]
    ident = np.eye(128, dtype=np.float32).astype(bf)
    triu = (j >= t).astype(np.float32).astype(bf)
    comp = (j < t).astype(np.float32).astype(bf)
    maskneg = np.where(j < t, 0.0, NEG).astype(np.float32).astype(bf)
    prev = (j >= t).astype(np.float32)
    cur = (j <= t).astype(np.float32)
    m01 = np.concatenate([cur, prev, cur, prev], axis=1).astype(bf)
    m01f = np.concatenate([np.zeros_like(prev), cur, prev, cur], axis=1).astype(bf)
    inv_freq = (10000.0 ** (-np.arange(0, HD, 2, dtype=np.float32) / HD)).astype(np.float32)
    ang = (np.arange(S, dtype=np.float32)[:, None] * inv_freq[None, :]).astype(np.float32)
    cos = np.cos(ang).astype(np.float32).T
    sin = np.sin(ang).astype(np.float32).T
    cos64 = np.concatenate([cos, cos], axis=0)
    sin64 = np.concatenate([-sin, sin], axis=0)
    cos2 = np.ascontiguousarray(np.concatenate([cos64, cos64], axis=0))
    sin2 = np.ascontiguousarray(np.concatenate([sin64, sin64], axis=0))
    shift = np.zeros((128, 64), np.float32)
    shift[np.arange(64) + 64, np.arange(64)] = 1.0
    return dict(c_ident=ident, c_triu=triu, c_comp=comp, c_maskneg=maskneg, c_m01=m01, c_m01f=m01f,
                c_cos=cos2, c_sin=sin2, c_shift=shift)


_CONST_SPECS = dict(c_ident=([128, 128], BF16), c_triu=([128, 128], BF16), c_comp=([128, 128], BF16),
                    c_maskneg=([128, 128], BF16), c_m01=([128, 512], BF16), c_m01f=([128, 512], BF16),
                    c_cos=([128, S], F32), c_sin=([128, S], F32), c_shift=([128, 64], F32))


def build_program(nseq=2, debug=False, stop_after=None):
    T = nseq * S
    nc = bass.Bass("TRN2", target_bir_lowering=False)
    I = {}

    def din(name, shape, dt=F32):
        I[name] = nc.dram_tensor(name, list(shape), dt, kind="ExternalInput").ap()

    din("x", [T, D])
    din("norm_mix", [DEPTH, D])
    din("w_qkv_a", [2, D, 3 * D])
    din("w_o_a", [2, D, D])
    din("norm_kv", [D])
    din("w_kv", [D, 2 * D])
    din("w_q_b", [2, D, D])
    din("w_o_b", [2, D, D])
    din("norm_ffn", [DEPTH, D])
    din("w_gate", [DEPTH, D, FF])
    din("w_up", [DEPTH, D, FF])
    din("w_down", [DEPTH, FF, D])
    din("norm_final", [D])
    for k, (shp, dt) in _CONST_SPECS.items():
        din(k, shp, dt)
    y = nc.dram_tensor("y", [T, D], F32, kind="ExternalOutput").ap()
    skind = "ExternalOutput" if debug else "Internal"

    def scratch(name, shape, dt):
        return nc.dram_tensor(name, list(shape), dt, kind=skind).ap()

    hbuf = scratch("s_h", [T, D], F32)
    hnT = scratch("s_hnT", [D, T], BF16)
    hkvT = scratch("s_hkvT", [D, T], BF16)
    qT = scratch("s_qT", [D, T], BF16)
    kT = scratch("s_kT", [D, T], BF16)
    vv = scratch("s_v", [T, D], BF16)
    oT = scratch("s_oT", [D, T], BF16)

    gst = ExitStack()
    with gst, nc.allow_low_precision("bf16 matmul operands, fp32 accumulate"):
        P = Prog(nc, gst)
        ps = [gst.enter_context(nc.psum_tensor(f"ps{i}", [128, 512], F32)) for i in range(8)]
        psb = [p.bitcast(BF16) for p in ps]
        ident = gst.enter_context(nc.sbuf_tensor("ident", [128, 128], BF16))
        rstd_eps = 1024.0 * EPS

        uid = [0]

        def T_(st, name, shape, dt):
            uid[0] += 1
            return st.enter_context(nc.sbuf_tensor(f"{name}_u{uid[0]}", list(shape), dt))

        first = [True]

        def load_ident():
            if first[0]:
                P.dma("sp", ident[:], I["c_ident"], writes=["ident"])
                first[0] = False

        def bcast_row(ap1d):
            return ap1d.partition_broadcast(128)

        class NormT:
            def __init__(self, st, tag, nb, ngain):
                self.tag = tag
                self.nb = nb
                self.hn = [T_(st, f"{tag}_hn{g}", [128, nb, D], BF16) for g in range(ngain)]
                self.junk = T_(st, f"{tag}_junk", [128, D], BF16) if ngain == 0 else None
                self.ss = T_(st, f"{tag}_ss", [128, 8], F32)
                self.rstd = T_(st, f"{tag}_rstd", [128, 8], F32)
                self.k = 0

            def stats(self, b, src, src_res):
                tag = self.tag
                ss, rstd = self.ss, self.rstd
                if self.junk is not None:
                    junk, jres = self.junk[:], f"{tag}_junk"
                else:
                    junk, jres = self.hn[0][:, b, :], f"{tag}_hn0_{b}"
                P.op("dve", lambda e: e.memset(ss[:, b:b + 1], 0.0), writes=[f"{tag}_ss{b}"])
                P.op("act", lambda e: e.activation(out=junk, in_=src, func=AF.Square, accum_out=ss[:, b:b + 1]),
                     reads=[src_res], writes=[jres, f"{tag}_ss{b}"])

            def finish(self):
                tag, nb = self.tag, self.nb
                ss, rstd = self.ss, self.rstd
                P.op("act", lambda e: e.activation(out=rstd[:, 0:nb], in_=ss[:, 0:nb], func=AF.Ln, bias=rstd_eps),
                     reads=[f"{tag}_ss{b}" for b in range(nb)], writes=[f"{tag}_rstd{b}" for b in range(nb)])
                P.op("act", lambda e: e.activation(out=rstd[:, 0:nb], in_=rstd[:, 0:nb], func=AF.Exp, scale=-0.5),
                     reads=[f"{tag}_rstd{b}" for b in range(nb)], writes=[f"{tag}_rstd{b}" for b in range(nb)])

            def scale(self, b, src, src_res, gi, gain, gain_res):
                tag = self.tag
                hn, rstd = self.hn[gi], self.rstd
                P.op("dve", lambda e: e.scalar_tensor_tensor(out=hn[:, b, :], in0=src, scalar=rstd[:, b:b + 1],
                                                              in1=gain[:], op0=ALU.mult, op1=ALU.mult),
                     reads=[src_res, f"{tag}_rstd{b}", gain_res], writes=[f"{tag}_hn{gi}_{b}"])

            def transpose_group(self, gi, fc, dst, dst_res, banks=(2, 3)):
                tag, nb = self.tag, self.nb
                hn = self.hn[gi]
                bank = banks[self.k % len(banks)]
                self.k += 1
                sres = f"ps{bank}"
                for b in range(nb):
                    P.op("pe", lambda e, b=b: e.transpose(
                        psb[bank][:, b * 128:(b + 1) * 128], hn[:, b, fc * 128:(fc + 1) * 128], ident[:]),
                        reads=[f"{tag}_hn{gi}_{b}", "ident"], writes=[sres])
                if fc % 2 == 0:
                    P.op("act", lambda e: e.copy(dst[:, fc, :], psb[bank][:, 0:nb * 128]),
                         reads=[sres], writes=[dst_res])
                else:
                    P.op("dve", lambda e: e.tensor_copy(dst[:, fc, :], psb[bank][:, 0:nb * 128]),
                         reads=[sres], writes=[dst_res])

            def transpose(self, gi, dst, dst_res, banks):
                tag, nb = self.tag, self.nb
                hn = self.hn[gi]
                for fc in range(NFC):
                    bank = banks[self.k % len(banks)]
                    self.k += 1
                    for b in range(nb):
                        P.op("pe", lambda e, b=b, fc=fc, bank=bank: e.transpose(
                            psb[bank][:, b * 128:(b + 1) * 128], hn[:, b, fc * 128:(fc + 1) * 128], ident[:]),
                            reads=[f"{tag}_hn{gi}_{b}", "ident"], writes=[f"ps{bank}"])
                    eng = "act" if fc % 2 == 0 else "dve"
                    if eng == "act":
                        P.op("act", lambda e, fc=fc, bank=bank: e.copy(dst[:, fc, :], psb[bank][:, 0:nb * 128]),
                             reads=[f"ps{bank}"], writes=[dst_res])
                    else:
                        P.op("dve", lambda e, fc=fc, bank=bank: e.tensor_copy(dst[:, fc, :], psb[bank][:, 0:nb * 128]),
                             reads=[f"ps{bank}"], writes=[dst_res])

        def fcview(ap2d):
            return ap2d.rearrange("(c p) n -> p c n", p=128)

        def load_w(st, name, src, nchunk, ncol, c0=0):
            w = T_(st, name, [128, nchunk, ncol], BF16)
            for c in range(nchunk):
                P.dma("pool", w[:, c, :], src[c * 128:(c + 1) * 128, c0:c0 + ncol], writes=[name])
            return w

        def load_w_swapped(st, name, src, c0):
            w = T_(st, name, [128, NFC, D], BF16)
            for c in range(NFC):
                sv = src[c * 128:(c + 1) * 128, c0:c0 + D].rearrange("k (h two d) -> k h two d", two=2, d=32)
                dv = w[:, c, :].rearrange("p (h two d) -> p h two d", two=2, d=32)
                P.dma("pool", dv[:, :, 0, :], sv[:, :, 1, :], writes=[name])
                P.dma("pool", dv[:, :, 1, :], sv[:, :, 0, :], writes=[name])
            return w

        def phase_pre():
            with ExitStack() as st:
                load_ident()
                g = T_(st, "pre_g", [128, D], F32)
                P.dma("sp", g[:], bcast_row(I["norm_mix"][0]), writes=["pre_g"])
                P.op("dve", lambda e: e.tensor_scalar(g[:], g[:], 32.0, None, ALU.mult), reads=["pre_g"], writes=["pre_g"])
                NB = 4
                nt = NormT(st, "pre", NB, 1)
                xb = [T_(st, f"pre_x{i}", [128, NB, D], F32) for i in range(2)]
                ob = [T_(st, f"pre_o{i}", [128, NFC, NB * 128], BF16) for i in range(2)]
                nchp = T // (NB * 128)

                def pre_load(c):
                    t0 = c * NB * 128
                    P.dma("sp", xb[c % 2][:], I["x"][t0:t0 + NB * 128, :].rearrange("(b p) d -> p b d", p=128),
                          writes=[f"pre_x{c % 2}"])

                pre_load(0)
                for c in range(nchp):
                    t0 = c * NB * 128
                    x_, o_ = xb[c % 2], ob[c % 2]
                    xr, orr = f"pre_x{c % 2}", f"pre_o{c % 2}"
                    if c + 1 < nchp:
                        pre_load(c + 1)
                    for b in range(NB):
                        nt.stats(b, x_[:, b, :], xr)
                    nt.finish()
                    for b in range(NB):
                        nt.scale(b, x_[:, b, :], xr, 0, g, "pre_g")
                    nt.transpose(0, o_, orr, [0, 1])
                    P.dma("sp", fcview(hnT)[:, :, t0:t0 + NB * 128], o_[:], reads=[orr])
                P.flush()

        def phase_qkv_a(L):
            with ExitStack() as st:
                W = load_w(st, "qa_w", I["w_qkv_a"][L], NFC, 3 * D)
                NB = 4
                TB = NB * 128
                hb = [T_(st, f"qa_h{i}", [128, NFC, TB], BF16) for i in range(2)]
                qo = [T_(st, f"qa_q{i}", [128, 8, TB], BF16) for i in range(2)]
                ko = [T_(st, f"qa_k{i}", [128, 8, TB], BF16) for i in range(2)]
                vo = [T_(st, f"qa_v{i}", [128, NB, D], BF16) for i in range(2)]
                k = 0

                def qa_load(c):
                    P.dma("sp", hb[c % 2][:], fcview(hnT)[:, :, c * TB:(c + 1) * TB], writes=[f"qa_h{c % 2}"])

                qa_load(0)
                for c in range(T // TB):
                    t0 = c * TB
                    i2 = c % 2
                    h_ = hb[i2]
                    hr = f"qa_h{i2}"
                    if c + 1 < T // TB:
                        qa_load(c + 1)
                    for which, dst, dres, scale in ((0, qo[i2], f"qa_q{i2}", 0.125), (1, ko[i2], f"qa_k{i2}", 1.0)):
                        for p in range(8):
                            bank = k % 4
                            k += 1
                            col = which * D + p * 128
                            for fc in range(NFC):
                                P.op("pe", lambda e, fc=fc, col=col, bank=bank, h_=h_: e.matmul(
                                    ps[bank][:, :], W[:, fc, col:col + 128], h_[:, fc, :],
                                    start=(fc == 0), stop=(fc == NFC - 1), skip_group_check=True),
                                    reads=["qa_w", hr], writes=[f"ps{bank}"])
                            if p % 2 == 0:
                                P.op("act", lambda e, p=p, bank=bank, dst=dst, scale=scale: e.activation(
                                    out=dst[:, p, :], in_=ps[bank][:, :], func=AF.Copy, scale=scale),
                                    reads=[f"ps{bank}"], writes=[dres])
                            else:
                                P.op("dve", lambda e, p=p, bank=bank, dst=dst, scale=scale: e.tensor_scalar(
                                    dst[:, p, :], ps[bank][:, :], scale, None, ALU.mult),
                                    reads=[f"ps{bank}"], writes=[dres])
                    P.dma("sp", fcview(qT)[:, :, t0:t0 + TB], qo[i2][:], reads=[f"qa_q{i2}"])
                    P.dma("sp", fcview(kT)[:, :, t0:t0 + TB], ko[i2][:], reads=[f"qa_k{i2}"])
                    v_ = vo[i2]
                    for b in range(NB):
                        for half in range(2):
                            bank = 4 + (k % 4)
                            k += 1
                            for fc in range(NFC):
                                P.op("pe", lambda e, fc=fc, b=b, half=half, bank=bank, h_=h_: e.matmul(
                                    ps[bank][:, :], h_[:, fc, b * 128:(b + 1) * 128],
                                    W[:, fc, 2 * D + half * 512:2 * D + (half + 1) * 512],
                                    start=(fc == 0), stop=(fc == NFC - 1), skip_group_check=True),
                                    reads=["qa_w", hr], writes=[f"ps{bank}"])
                            if half == 0:
                                P.op("act", lambda e, b=b, half=half, bank=bank, v_=v_: e.copy(
                                    v_[:, b, half * 512:(half + 1) * 512], ps[bank][:, :]),
                                    reads=[f"ps{bank}"], writes=[f"qa_v{i2}"])
                            else:
                                P.op("dve", lambda e, b=b, half=half, bank=bank, v_=v_: e.tensor_copy(
                                    v_[:, b, half * 512:(half + 1) * 512], ps[bank][:, :]),
                                    reads=[f"ps{bank}"], writes=[f"qa_v{i2}"])
                    P.dma("sp", vv[t0:t0 + TB, :].rearrange("(b p) n -> p b n", p=128), v_[:], reads=[f"qa_v{i2}"])
                P.flush()

        def phase_att_a():
            with ExitStack() as st:
                triu = T_(st, "aa_triu", [128, 128], BF16)
                comp = T_(st, "aa_comp", [128, 128], BF16)
                mneg = T_(st, "aa_mneg", [128, 128], BF16)
                P.dma("sp", triu[:], I["c_triu"], writes=["aa_triu"])
                P.dma("sp", comp[:], I["c_comp"], writes=["aa_comp"])
                P.dma("sp", mneg[:], I["c_maskneg"], writes=["aa_mneg"])
                load_ident()
                NS = 2
                NBUF = 2
                qs = [[T_(st, f"aa_q{s}_{i}", [64, S], BF16) for i in range(NBUF)] for s in range(NS)]
                ks = [[T_(st, f"aa_k{s}_{i}", [64, S], BF16) for i in range(NBUF)] for s in range(NS)]
                vs = [[T_(st, f"aa_v{s}_{i}", [128, S // 128, HD], BF16) for i in range(NBUF)] for s in range(NS)]
                os_ = [[T_(st, f"aa_o{s}_{i}", [64, S], BF16) for i in range(NBUF)] for s in range(NS)]
                NE, NL, NX, NP_ = 6, 5, 3, 3
                Eb = [T_(st, f"aa_E{i}", [128, 512], F32) for i in range(NE)]
                Lb = [T_(st, f"aa_L{i}", [128, 512], BF16) for i in range(NL)]
                Xb = [T_(st, f"aa_X{i}", [128, 512], F32) for i in range(NX)]
                Pb = [T_(st, f"aa_P{i}", [128, 512], BF16) for i in range(NP_)]
                tiles = []
                jobs = []
                for sq in range(nseq):
                    for hp in range(H // NS):
                        jobs.append([(sq, hp * NS + s) for s in range(NS)])
                tix = [0]

                def mk_tile(s, bi, sq, h, g, kb, first_in_group, last_in_group):
                    ti = tix[0]
                    tix[0] += 1
                    q_, k_, v_, o_ = qs[s][bi], ks[s][bi], vs[s][bi], os_[s][bi]
                    qr, kr, vr, orr = f"aa_q{s}_{bi}", f"aa_k{s}_{bi}", f"aa_v{s}_{bi}", f"aa_o{s}_{bi}"
                    i_d = kb - 4 * g
                    c0 = 128 * i_d if i_d >= 0 else 0
                    diag = i_d >= 0
                    sb = (0, 1, 6, 7)[ti % 4]
                    cc = 2 + s
                    ob = 4 + s
                    E, L, X, Pt = Eb[ti % NE], Lb[ti % NL], Xb[ti % NX], Pb[ti % NP_]
                    Er, Lr, Xr, Pr = f"aa_E{ti % NE}", f"aa_L{ti % NL}", f"aa_X{ti % NX}", f"aa_P{ti % NP_}"
                    q0 = g * 512

                    def s0():
                        P.op("pe", lambda e: e.matmul(ps[sb][:, c0:512], k_[:, kb * 128:(kb + 1) * 128],
                                                      q_[:, q0 + c0:q0 + 512], start=True, stop=not diag,
                                                      skip_group_check=True),
                             reads=[qr, kr], writes=[f"ps{sb}"])
                        if diag:
                            P.op("pe", lambda e: e.matmul(ps[sb][:, c0:c0 + 128], ident[:], mneg[:],
                                                          start=False, stop=True, skip_group_check=True),
                                 reads=["ident", "aa_mneg"], writes=[f"ps{sb}"])

                    def s1():
                        P.op("act", lambda e: e.activation(out=E[:, c0:512], in_=ps[sb][:, c0:512], func=AF.Exp),
                             reads=[f"ps{sb}"], writes=[Er])

                    def s2():
                        P.op("act", lambda e: e.activation(out=L[:, c0:512], in_=E[:, c0:512], func=AF.Ln, bias=1.0),
                             reads=[Er], writes=[Lr])

                    def s3():
                        P.op("pe", lambda e: e.matmul(ps[cc][:, c0:512], triu[:], L[:, c0:512],
                                                      start=first_in_group, stop=False, skip_group_check=True),
                             reads=[Lr, "aa_triu"], writes=[f"ps{cc}"])

                    def s4():
                        P.op("act", lambda e: e.activation(out=X[:, c0:512], in_=ps[cc][:, c0:512], func=AF.Exp, scale=-1.0),
                             reads=[f"ps{cc}"], writes=[Xr])

                    def s5():
                        P.op("pe", lambda e: e.matmul(ps[cc][:, c0:512], comp[:], L[:, c0:512],
                                                      start=False, stop=last_in_group, skip_group_check=True),
                             reads=[Lr, "aa_comp"], writes=[f"ps{cc}"])
                        P.op("dve", lambda e: e.tensor_tensor(out=Pt[:, c0:512], in0=E[:, c0:512], in1=X[:, c0:512], op=ALU.mult),
                             reads=[Er, Xr], writes=[Pr])

                    def s6():
                        P.op("pe", lambda e: e.matmul(ps[ob][0:64, c0:512], v_[:, kb, :], Pt[:, c0:512],
                                                      start=first_in_group, stop=last_in_group, skip_group_check=True),
                             reads=[Pr, vr], writes=[f"ps{ob}"])
                        if last_in_group:
                            P.op("dve", lambda e: e.tensor_copy(o_[:, q0:q0 + 512], ps[ob][0:64, :]),
                                 reads=[f"ps{ob}"], writes=[orr])
                            if g == S // 512 - 1:
                                tb = sq * S
                                P.dma("sp", oT[h * HD:(h + 1) * HD, tb:tb + S], o_[:], reads=[orr])

                    return [s0, s1, s2, s3, s4, s5, s6]

                def loads(ji):
                    bi = ji % NBUF
                    for s in range(NS):
                        sq, h = jobs[ji][s]
                        tb = sq * S
                        P.dma("sp", qs[s][bi][:], qT[h * HD:(h + 1) * HD, tb:tb + S], writes=[f"aa_q{s}_{bi}"])
                        P.dma("sp", ks[s][bi][:], kT[h * HD:(h + 1) * HD, tb:tb + S], writes=[f"aa_k{s}_{bi}"])
                        P.dma("sp", vs[s][bi][:], vv[tb:tb + S, h * HD:(h + 1) * HD].rearrange("(b p) d -> p b d", p=128),
                              writes=[f"aa_v{s}_{bi}"])

                NST = 7
                pending = []

                def emit_iteration(new_tile):
                    pending.insert(0, new_tile)
                    for sidx in reversed(range(len(pending))):
                        tl = pending[sidx]
                        if tl is not None and sidx < NST:
                            tl[sidx]()
                    if len(pending) >= NST:
                        pending.pop()

                loads(0)
                for ji in range(len(jobs)):
                    bi = ji % NBUF
                    nit = 0
                    for g in range(S // 512):
                        kbs = list(range(4 * g + 3, -1, -1))
                        for n_, kb in enumerate(kbs):
                            for s in range(NS):
                                sq, h = jobs[ji][s]
                                emit_iteration(mk_tile(s, bi, sq, h, g, kb, n_ == 0, n_ == len(kbs) - 1))
                                nit += 1
                                if nit == NST + 1 and ji + 1 < len(jobs):
                                    loads(ji + 1)
                for _ in range(NST):
                    emit_iteration(None)
                P.flush()

        def phase_mlp(L):
            with ExitStack() as st:
                wo_src = I["w_o_a"][L] if L < 2 else I["w_o_b"][L - 2]
                Wo = load_w(st, "ml_wo", wo_src, NFC, D)
                Wg = load_w(st, "ml_wg", I["w_gate"][L], NFC, FF)
                Wu = load_w(st, "ml_wu", I["w_up"][L], NFC, FF)
                Wd = load_w(st, "ml_wd", I["w_down"][L], NJ, D)
                load_ident()

                def gain(name, row):
                    g = T_(st, name, [128, D], F32)
                    P.dma("sp", g[:], bcast_row(row), writes=[name])
                    P.op("dve", lambda e: e.tensor_scalar(g[:], g[:], 32.0, None, ALU.mult), reads=[name], writes=[name])
                    return g

                g_ffn = gain("ml_gf", I["norm_ffn"][L])
                if L < 3:
                    g_next = [("ml_gn", gain("ml_gn", I["norm_mix"][L + 1]), hnT)]
                    if L == 1:
                        g_next.insert(0, ("ml_gk", gain("ml_gk", I["norm_kv"]), hkvT))
                else:
                    g_next = [("ml_gn", gain("ml_gn", I["norm_final"]), None)]
                NB = 2
                TB = NB * 128
                n1 = NormT(st, "m1", NB, 1)
                n2 = NormT(st, "m2", NB, 1 if L < 3 else 0)
                hc = [T_(st, f"ml_h{i}", [128, NB, D], F32) for i in range(2)]
                oc = [T_(st, f"ml_o{i}", [128, NFC, TB], BF16) for i in range(2)]
                hT2 = [T_(st, f"ml_hT2_{i}", [128, NFC, TB], BF16) for i in range(2)]
                hTn = T_(st, "ml_hTn", [128, NFC, TB], BF16) if L < 3 else None
                sg = [T_(st, f"ml_sg{i}", [128, TB], BF16 if L == 1 else F32) for i in range(2)]
                at = [T_(st, f"ml_at{i}", [128, TB], BF16) for i in range(3)]
                yo = T_(st, "ml_y", [128, NB, D], F32) if L == 3 else None
                hsrc = I["x"] if L == 0 else hbuf
                nch = T // TB
                kk = [0]
                jc = [0]
                pend = [None]

                def hres(i):
                    return [f"ml_h{i}_{b}" for b in range(NB)]

                def stage_A_load(c):
                    i = c % 2
                    t0 = c * TB
                    P.dma("sp", hc[i][:], hsrc[t0:t0 + TB, :].rearrange("(b p) d -> p b d", p=128), writes=hres(i))
                    P.dma("sp", oc[i][:], fcview(oT)[:, :, t0:t0 + TB], writes=[f"ml_o{i}"])

                def stage_A(c):
                    i = c % 2
                    h_, o_ = hc[i], oc[i]
                    orr = f"ml_o{i}"
                    for b in range(NB):
                        for half in range(2):
                            bank = 2 + (b * 2 + half) % 2
                            for fc in range(NFC):
                                P.op("pe", lambda e, fc=fc, b=b, half=half, bank=bank, o_=o_: e.matmul(
                                    ps[bank][:, :], o_[:, fc, b * 128:(b + 1) * 128], Wo[:, fc, half * 512:(half + 1) * 512],
                                    start=(fc == 0), stop=(fc == NFC - 1), skip_group_check=True),
                                    reads=[orr, "ml_wo"], writes=[f"ps{bank}"])
                            P.op("dve", lambda e, b=b, half=half, bank=bank, h_=h_: e.tensor_tensor(
                                out=h_[:, b, half * 512:(half + 1) * 512], in0=h_[:, b, half * 512:(half + 1) * 512],
                                in1=ps[bank][:, :], op=ALU.add),
                                reads=[f"ps{bank}", f"ml_h{i}_{b}"], writes=[f"ml_h{i}_{b}"])

                def stage_A_norm(c, part=None):
                    i = c % 2
                    h_ = hc[i]
                    for b in range(NB):
                        if part is None or part == b:
                            n1.stats(b, h_[:, b, :], f"ml_h{i}_{b}")
                    if part is None or part == 2:
                        n1.finish()
                    if part is None or part == 3:
                        for b in range(NB):
                            n1.scale(b, h_[:, b, :], f"ml_h{i}_{b}", 0, g_ffn, "ml_gf")

                def stage_T1(c):
                    i = c % 2
                    for fc in range(NFC):
                        n1.transpose_group(0, fc, hT2[i], f"ml_hT2_{i}")

                def emit_down(a_, ar, j):
                    for b in range(NB):
                        for half in range(2):
                            dbank = 4 + b * 2 + half
                            P.op("pe", lambda e, b=b, half=half, dbank=dbank, a_=a_, j=j: e.matmul(
                                ps[dbank][:, :], a_[:, b * 128:(b + 1) * 128], Wd[:, j, half * 512:(half + 1) * 512],
                                start=(j == 0), stop=(j == NJ - 1), skip_group_check=True),
                                reads=[ar, "ml_wd"], writes=[f"ps{dbank}"])

                def stage_F(c, j0, j1, hook=None):
                    i = c % 2
                    hT = hT2[i]
                    hTr = f"ml_hT2_{i}"
                    for j in range(j0, j1):
                        if hook is not None:
                            hook(j)
                        bank = kk[0] % 2
                        kk[0] += 1
                        a_ = at[jc[0] % 3]
                        ar = f"ml_at{jc[0] % 3}"
                        s_ = sg[jc[0] % 2]
                        sr = f"ml_sg{jc[0] % 2}"
                        jc[0] += 1
                        for gi, Wx in enumerate((Wg, Wu)):
                            for fc in range(NFC):
                                P.op("pe", lambda e, fc=fc, gi=gi, Wx=Wx, bank=bank, j=j, hT=hT: e.matmul(
                                    ps[bank][:, gi * TB:(gi + 1) * TB], Wx[:, fc, j * 128:(j + 1) * 128], hT[:, fc, :],
                                    start=(fc == 0), stop=(fc == NFC - 1), skip_group_check=True),
                                    reads=[hTr, "ml_wg" if gi == 0 else "ml_wu"], writes=[f"ps{bank}"])
                        P.op("act", lambda e, bank=bank, s_=s_: e.activation(out=s_[:], in_=ps[bank][:, 0:TB], func=AF.Silu),
                             reads=[f"ps{bank}"], writes=[sr])
                        P.op("dve", lambda e, bank=bank, s_=s_, a_=a_: e.tensor_tensor(
                            out=a_[:], in0=s_[:], in1=ps[bank][:, TB:2 * TB], op=ALU.mult),
                            reads=[f"ps{bank}", sr], writes=[ar])
                        pend.append((a_, ar, j))
                        if len(pend) > 3:
                            emit_down(*pend.pop(1))
                    if j1 == NJ:
                        while len(pend) > 1:
                            emit_down(*pend.pop(1))

                def norm_out(gname, gt, dst, transposes_now):
                    pass

                def stage_R(c, part=None):
                    i = c % 2
                    t0 = c * TB
                    h_ = hc[i]
                    if part is None or part == 0:
                        for b in range(NB):
                            for half in range(2):
                                dbank = 4 + b * 2 + half
                                P.op("dve", lambda e, b=b, half=half, dbank=dbank, h_=h_: e.tensor_tensor(
                                    out=h_[:, b, half * 512:(half + 1) * 512], in0=h_[:, b, half * 512:(half + 1) * 512],
                                    in1=ps[dbank][:, :], op=ALU.add),
                                    reads=[f"ps{dbank}", f"ml_h{i}_{b}"], writes=[f"ml_h{i}_{b}"])
                        if L < 3:
                            P.dma("sp", hbuf[t0:t0 + TB, :].rearrange("(b p) d -> p b d", p=128), h_[:], reads=hres(i))
                    for b in range(NB):
                        if part is None or part == b:
                            n2.stats(b, h_[:, b, :], f"ml_h{i}_{b}")
                    if part is None or part == 2:
                        n2.finish()
                    if part is None or part == 3:
                        if L < 3:
                            for gi, (gname, gt, dst) in enumerate(g_next):
                                for b in range(NB):
                                    n2.scale(b, h_[:, b, :], f"ml_h{i}_{b}", 0, gt, gname)
                                if gi < len(g_next) - 1:
                                    for fc in range(NFC):
                                        n2.transpose_group(0, fc, hTn, "ml_hTn")
                                    P.dma("sp", fcview(dst)[:, :, t0:t0 + TB], hTn[:], reads=["ml_hTn"])
                        else:
                            gt = g_next[0][1]
                            rstd = n2.rstd
                            for b in range(NB):
                                P.op("dve", lambda e, b=b, h_=h_: e.scalar_tensor_tensor(
                                    out=yo[:, b, :], in0=h_[:, b, :], scalar=rstd[:, b:b + 1], in1=gt[:],
                                    op0=ALU.mult, op1=ALU.mult),
                                    reads=[f"ml_h{i}_{b}", f"m2_rstd{b}", "ml_gn"], writes=[f"ml_y{b}"])
                            P.dma("sp", y[t0:t0 + TB, :].rearrange("(b p) d -> p b d", p=128), yo[:],
                                  reads=[f"ml_y{b}" for b in range(NB)])

                def stage_T2_group(c, fc):
                    if L == 3:
                        return
                    n2.transpose_group(0, fc, hTn, "ml_hTn")
                    if fc == NFC - 1:
                        t0 = c * TB
                        dst = g_next[-1][2]
                        P.dma("sp", fcview(dst)[:, :, t0:t0 + TB], hTn[:], reads=["ml_hTn"])

                stage_A_load(0)
                stage_A(0)
                stage_A_norm(0)
                stage_T1(0)
                if nch > 1:
                    stage_A_load(1)
                for c in range(nch):
                    def hook(j, c=c):
                        if 1 <= j <= 3 and c > 0:
                            stage_R(c - 1, j)
                            if j == 3 and c + 1 < nch:
                                stage_A_load(c + 1)
                        if j == 7 and c + 1 < nch:
                            stage_A(c + 1)
                        if 9 <= j <= 12 and c + 1 < nch:
                            stage_A_norm(c + 1, j - 9)
                        if 6 <= j < 14 and c > 0:
                            stage_T2_group(c - 1, j - 6)
                        if 14 <= j < 22 and c + 1 < nch:
                            n1.transpose_group(0, j - 14, hT2[(c + 1) % 2], f"ml_hT2_{(c + 1) % 2}")
                    stage_F(c, 0, NJ, hook)
                    stage_R(c, 0)
                for part in (1, 2, 3):
                    stage_R(nch - 1, part)
                for fc in range(NFC):
                    stage_T2_group(nch - 1, fc)
                P.flush(final=(L == 3))

        def phase_qkv_b(L):
            j = L - 2
            with ExitStack() as st:
                Wq = load_w(st, "qb_wq", I["w_q_b"][j], NFC, D)
                Wqs = load_w_swapped(st, "qb_wqs", I["w_q_b"][j], 0)
                do_kv = (L == 2)
                if do_kv:
                    Wk = load_w(st, "qb_wk", I["w_kv"], NFC, D, 0)
                    Wks = load_w_swapped(st, "qb_wks", I["w_kv"], 0)
                    Wv = load_w(st, "qb_wv", I["w_kv"], NFC, D, D)
                cosb = T_(st, "qb_cos", [128, S], F32)
                sinb = T_(st, "qb_sin", [128, S], F32)
                P.dma("sp", cosb[:], I["c_cos"], writes=["qb_cos"])
                P.dma("sp", sinb[:], I["c_sin"], writes=["qb_sin"])
                NB = 4
                TB = NB * 128
                hb = [T_(st, f"qb_h{i}", [128, NFC, TB], BF16) for i in range(2)]
                kb_ = [T_(st, f"qb_hk{i}", [128, NFC, TB], BF16) for i in range(2)] if do_kv else None
                qo = T_(st, "qb_q", [128, 8, TB], BF16)
                ko = T_(st, "qb_k", [128, 8, TB], BF16) if do_kv else None
                vo = T_(st, "qb_v", [128, NB, D], BF16) if do_kv else None
                t1 = [T_(st, f"qb_t1_{i}", [128, TB], F32) for i in range(2)]
                t2 = [T_(st, f"qb_t2_{i}", [128, TB], F32) for i in range(2)]
                k = 0

                def qb_load(c):
                    P.dma("sp", hb[c % 2][:], fcview(hnT)[:, :, c * TB:(c + 1) * TB], writes=[f"qb_h{c % 2}"])
                    if do_kv:
                        P.dma("sp", kb_[c % 2][:], fcview(hkvT)[:, :, c * TB:(c + 1) * TB], writes=[f"qb_hk{c % 2}"])

                for c in range(T // TB):
                    t0 = c * TB
                    p0 = t0 % S
                    i2 = c % 2
                    h_ = hb[i2]
                    hr = f"qb_h{i2}"
                    if c == 0:
                        qb_load(0)
                    if c + 1 < T // TB:
                        qb_load(c + 1)
                    srcs = [(h_, hr, Wq, Wqs, "qb_wq", "qb_wqs", qo, "qb_q", 0.125)]
                    if do_kv:
                        hk_ = kb_[i2]
                        hkr = f"qb_hk{i2}"
                        srcs.append((hk_, hkr, Wk, Wks, "qb_wk", "qb_wks", ko, "qb_k", 1.0))
                    for (a_, ar, W1, W2, w1r, w2r, dst, dres, scale) in srcs:
                        for p in range(8):
                            bA = (k % 2) * 2
                            bB = bA + 1
                            ta, tbb = t1[k % 2], t2[k % 2]
                            tar, tbr = f"qb_t1_{k % 2}", f"qb_t2_{k % 2}"
                            k += 1
                            for (bank, Wx, wr) in ((bA, W1, w1r), (bB, W2, w2r)):
                                for fc in range(NFC):
                                    P.op("pe", lambda e, fc=fc, bank=bank, Wx=Wx, p=p, a_=a_: e.matmul(
                                        ps[bank][:, :], Wx[:, fc, p * 128:(p + 1) * 128], a_[:, fc, :],
                                        start=(fc == 0), stop=(fc == NFC - 1), skip_group_check=True),
                                        reads=[ar, wr], writes=[f"ps{bank}"])
                            P.op("dve", lambda e, bA=bA, ta=ta, scale=scale, p0=p0: e.scalar_tensor_tensor(
                                out=ta[:], in0=ps[bA][:, :], scalar=scale, in1=cosb[:, p0:p0 + TB], op0=ALU.mult, op1=ALU.mult),
                                reads=[f"ps{bA}", "qb_cos"], writes=[tar])
                            P.op("dve", lambda e, bB=bB, tbb=tbb, scale=scale, p0=p0: e.scalar_tensor_tensor(
                                out=tbb[:], in0=ps[bB][:, :], scalar=scale, in1=sinb[:, p0:p0 + TB], op0=ALU.mult, op1=ALU.mult),
                                reads=[f"ps{bB}", "qb_sin"], writes=[tbr])
                            P.op("pool", lambda e, ta=ta, tbb=tbb, dst=dst, p=p: e.tensor_tensor(
                                out=dst[:, p, :], in0=ta[:], in1=tbb[:], op=ALU.add),
                                reads=[tar, tbr], writes=[dres])
                    P.dma("sp", fcview(qT)[:, :, t0:t0 + TB], qo[:], reads=["qb_q"])
                    if do_kv:
                        P.dma("sp", fcview(kT)[:, :, t0:t0 + TB], ko[:], reads=["qb_k"])
                        for b in range(NB):
                            for half in range(2):
                                bank = 4 + (k % 4)
                                k += 1
                                for fc in range(NFC):
                                    P.op("pe", lambda e, fc=fc, b=b, half=half, bank=bank, hk_=hk_: e.matmul(
                                        ps[bank][:, :], hk_[:, fc, b * 128:(b + 1) * 128],
                                        Wv[:, fc, half * 512:(half + 1) * 512],
                                        start=(fc == 0), stop=(fc == NFC - 1), skip_group_check=True),
                                        reads=["qb_wv", hkr], writes=[f"ps{bank}"])
                                P.op("act", lambda e, b=b, half=half, bank=bank: e.copy(
                                    vo[:, b, half * 512:(half + 1) * 512], ps[bank][:, :]),
                                    reads=[f"ps{bank}"], writes=["qb_v"])
                        P.dma("sp", vv[t0:t0 + TB, :].rearrange("(b p) n -> p b n", p=128), vo[:], reads=["qb_v"])
                P.flush()

        def phase_att_b():
            with ExitStack() as st:
                m01 = T_(st, "ab_m01", [128, 512], BF16)
                m01f = T_(st, "ab_m01f", [128, 512], BF16)
                zk = T_(st, "ab_zk", [64, 128], BF16)
                shiftm = T_(st, "ab_shift", [128, 64], F32)
                P.dma("sp", m01[:], I["c_m01"], writes=["ab_m01"])
                P.dma("sp", m01f[:], I["c_m01f"], writes=["ab_m01f"])
                P.dma("sp", shiftm[:], I["c_shift"], writes=["ab_shift"])
                P.op("dve", lambda e: e.memset(zk[:], 0.0), writes=["ab_zk"])
                NS = 2
                qs = [T_(st, f"ab_q{s}", [64, S], BF16) for s in range(NS)]
                ks = [T_(st, f"ab_k{s}", [64, S], BF16) for s in range(NS)]
                vs = [[[T_(st, f"ab_v{s}_{i}_{r}", [128, S // 128, 128], BF16) for r in range(3)] for i in range(2)]
                      for s in range(NS)]
                for s in range(NS):
                    for i in range(2):
                        for r in range(3):
                            P.op("pool", lambda e, vt=vs[s][i][r]: e.memset(vt[:, :, 64:128], 1.0),
                                 writes=[f"ab_v{s}_{i}_{r}"])
                nd = [T_(st, f"ab_nd{s}", [128, S], F32) for s in range(NS)]
                ot = [T_(st, f"ab_ot{s}", [64, S], BF16) for s in range(NS)]
                NPB = 4
                Pb = [T_(st, f"ab_P{i}", [128, 512], BF16) for i in range(NPB)]
                jobs = []
                for sq in range(nseq):
                    for hp in range(H // NS):
                        jobs.append([(sq, hp * NS + s) for s in range(NS)])
                tix = [0]

                def mk_tile(s, vb, sq, h, ri, r, c, kb0, first_tile, last_tile):
                    ti = tix[0]
                    tix[0] += 1
                    q_, k_ = qs[s], ks[s]
                    v_ = vs[s][vb][ri]
                    qr, kr, vr = f"ab_q{s}", f"ab_k{s}", f"ab_v{s}_{vb}_{ri}"
                    n = S // r
                    nbc = n // 128
                    last_pair = (kb0 + 2 == nbc)
                    sbank = ti % 4
                    nbank = 4 + ti % 4
                    Pt = Pb[ti % NPB]
                    Pr = f"ab_P{ti % NPB}"
                    ndr = f"ab_nd{s}"
                    n2 = 128 if last_pair else 256
                    ncols = 256 + n2

                    def tsl(nb, cnt):
                        a = c + r * 128 * nb
                        return slice(a, a + (cnt - 1) * r + 1, r)

                    def s0():
                        P.op("pe", lambda e: e.matmul(ps[sbank][:, 0:256], k_[:, tsl(kb0, 128)], q_[:, tsl(kb0, 256)],
                                                      start=True, stop=True, skip_group_check=True),
                             reads=[qr, kr], writes=[f"ps{sbank}"])
                        P.op("pe", lambda e: e.matmul(ps[sbank][:, 256:256 + n2], k_[:, tsl(kb0 + 1, 128)],
                                                      q_[:, tsl(kb0 + 1, n2)],
                                                      start=True, stop=True, skip_group_check=True),
                             reads=[qr, kr], writes=[f"ps{sbank}"])

                    def s1():
                        P.op("act", lambda e: e.activation(out=Pt[:, 0:ncols], in_=ps[sbank][:, 0:ncols], func=AF.Exp),
                             reads=[f"ps{sbank}"], writes=[Pr])

                    def s2():
                        P.op("pool" if s == 0 else "dve",
                             lambda e: e.tensor_tensor(out=Pt[:, 0:ncols], in0=Pt[:, 0:ncols], in1=m01[:, 0:ncols],
                                                       op=ALU.mult),
                             reads=[Pr, "ab_m01"], writes=[Pr])

                    def s3():
                        P.op("pe", lambda e: e.matmul(ps[nbank][:, 0:256], v_[:, c * nbc + kb0, :], Pt[:, 0:256],
                                                      start=True, stop=False, skip_group_check=True),
                             reads=[Pr, vr], writes=[f"ps{nbank}"])
                        P.op("pe", lambda e: e.matmul(ps[nbank][:, 128:128 + n2], v_[:, c * nbc + kb0 + 1, :],
                                                      Pt[:, 256:256 + n2],
                                                      start=False, stop=True, skip_group_check=True),
                             reads=[Pr, vr], writes=[f"ps{nbank}"])

                    def s4():
                        nq = 128 + n2
                        sl = tsl(kb0, nq)
                        if first_tile:
                            P.op("pool", lambda e: e.memset(nd[s][:, :], 0.0), writes=[ndr])
                        P.op("dve", lambda e: e.tensor_tensor(out=nd[s][:, sl], in0=nd[s][:, sl],
                                                              in1=ps[nbank][:, 0:nq], op=ALU.add),
                             reads=[f"ps{nbank}", ndr], writes=[ndr])
                        if last_tile:
                            orr = f"ab_ot{s}"
                            P.op("act", lambda e: e.activation(out=nd[s][64:128, :], in_=nd[s][64:128, :], func=AF.Ln),
                                 reads=[ndr], writes=[ndr])
                            P.op("act", lambda e: e.activation(out=nd[s][64:128, :], in_=nd[s][64:128, :], func=AF.Exp, scale=-1.0),
                                 reads=[ndr], writes=[ndr])
                            for ch in range(S // 512):
                                fb = 4 + (ti + 1 + ch) % 4
                                P.op("pe", lambda e, ch=ch, fb=fb: e.matmul(
                                    ps[fb][0:64, :], shiftm[:, :], nd[s][:, ch * 512:(ch + 1) * 512],
                                    start=True, stop=True, skip_group_check=True),
                                    reads=[ndr, "ab_shift"], writes=[f"ps{fb}"])
                                P.op("dve", lambda e, ch=ch, fb=fb: e.tensor_tensor(
                                    out=ot[s][:, ch * 512:(ch + 1) * 512], in0=nd[s][0:64, ch * 512:(ch + 1) * 512],
                                    in1=ps[fb][0:64, :], op=ALU.mult),
                                    reads=[f"ps{fb}", ndr], writes=[orr])
                            tb = sq * S
                            P.dma("sp", oT[h * HD:(h + 1) * HD, tb:tb + S], ot[s][:], reads=[orr])

                    return [s0, s1, s2, s3, s4]

                def loads_qk(ji):
                    for s in range(NS):
                        sq, h = jobs[ji][s]
                        tb = sq * S
                        P.dma("sp", qs[s][:], qT[h * HD:(h + 1) * HD, tb:tb + S], writes=[f"ab_q{s}"])
                        P.dma("sp", ks[s][:], kT[h * HD:(h + 1) * HD, tb:tb + S], writes=[f"ab_k{s}"])

                def loads_v(ji):
                    vb = ji % 2
                    for s in range(NS):
                        sq, h = jobs[ji][s]
                        tb = sq * S
                        for ri, r in enumerate(BRANCHES):
                            nbc = S // r // 128
                            vsrc = vv[tb:tb + S, h * HD:(h + 1) * HD]
                            vt = vs[s][vb][ri]
                            for c in range(r):
                                src = vsrc.rearrange("(nb i c) d -> c i nb d", i=128, c=r)[c]
                                P.dma("sp", vt[:, c * nbc:(c + 1) * nbc, 0:64], src, writes=[f"ab_v{s}_{vb}_{ri}"])

                NST = 5
                pending = []

                def emit_iteration(new_tile):
                    pending.insert(0, new_tile)
                    for sidx in reversed(range(len(pending))):
                        tl = pending[sidx]
                        if tl is not None and sidx < NST:
                            tl[sidx]()
                    if len(pending) >= NST:
                        pending.pop()

                loads_v(0)
                for ji in range(len(jobs)):
                    vb = ji % 2
                    nit = 0
                    loads_qk(ji)
                    tl = []
                    for ri, r in enumerate(BRANCHES):
                        nbc = S // r // 128
                        for c in range(r):
                            for nb0 in range(0, nbc, 2):
                                tl.append((ri, r, c, nb0))
                    for n_, (ri, r, c, nb0) in enumerate(tl):
                        for s in range(NS):
                            sq, h = jobs[ji][s]
                            emit_iteration(mk_tile(s, vb, sq, h, ri, r, c, nb0, n_ == 0, n_ == len(tl) - 1))
                            nit += 1
                            if nit == NST + 1 and ji + 1 < len(jobs):
                                loads_v(ji + 1)
                for _ in range(NST):
                    emit_iteration(None)
                P.flush()

        seq = [("pre", phase_pre)]
        for L in range(DEPTH):
            if L < 2:
                seq.append((f"qkv{L}", lambda L=L: phase_qkv_a(L)))
                seq.append((f"att{L}", phase_att_a))
            else:
                seq.append((f"qkv{L}", lambda L=L: phase_qkv_b(L)))
                seq.append((f"att{L}", phase_att_b))
            seq.append((f"mlp{L}", lambda L=L: phase_mlp(L)))
        for name, fn in seq:
            fn()
            if stop_after == name:
                P.flush(final=True)
                break
    return nc


_NC_CACHE = {}


def kernel(**inputs):
    nseq = 2
    if "nc" not in _NC_CACHE:
        _NC_CACHE["nc"] = build_program(nseq=nseq)
    nc = _NC_CACHE["nc"]
    consts = _consts()
    x = np.ascontiguousarray(np.asarray(inputs["x"], dtype=np.float32))
    B = x.shape[0]
    per = B // NCORES
    in_maps = []
    for c in range(NCORES):
        m = {"x": np.ascontiguousarray(x[c * per:(c + 1) * per].reshape(per * S, D))}
        for k in ("norm_mix", "w_qkv_a", "w_o_a", "norm_kv", "w_kv", "w_q_b", "w_o_b", "norm_ffn",
                  "w_gate", "w_up", "w_down", "norm_final"):
            m[k] = np.ascontiguousarray(np.asarray(inputs[k], dtype=np.float32))
        m.update(consts)
        in_maps.append(m)
    res = run_bass_kernel_spmd(nc, in_maps, core_ids=list(range(NCORES)))
    out = np.concatenate([np.asarray(r["y"]).reshape(per, S, D) for r in res.results], axis=0)
    return out.astype(np.float32)
```
